# Optimizing a Trainium2 kernel written in Bass

```python
import jax, jax.numpy as jnp
from jax import lax
import numpy as np

D_MODEL = 1024
BATCH = 32
SEQ = 2048
DEPTH = 2

N_Q_HEADS = 8
N_KV_HEADS = 2
HEAD_DIM = 64
WINDOW = 128
ATTN_BLOCK = 128
ATTN_WIDTH = N_Q_HEADS * HEAD_DIM
KV_WIDTH = N_KV_HEADS * HEAD_DIM
CONV_WIDTH = D_MODEL - ATTN_WIDTH
CONV_KERNEL = 31
IN0_WIDTH = ATTN_WIDTH + 2 * KV_WIDTH + 2 * CONV_WIDTH
POOL_WINDOWS = (2, 4, 8, 16)
POOL_WIDTH = D_MODEL // 2
POOL_GROUP = POOL_WIDTH // len(POOL_WINDOWS)
SGU_WIDTH = D_MODEL - POOL_WIDTH
SGU_HEADS = 4
SGU_HEAD_DIM = SGU_WIDTH // SGU_HEADS
SGU_CHUNK = 128
IN1_WIDTH = POOL_WIDTH + 2 * SGU_WIDTH
D_FF = -(-(8 * D_MODEL) // (3 * 256)) * 256
N_EVEN = (DEPTH + 1) // 2
N_ODD = DEPTH // 2
EPS = 1e-5

kernel_name = "hybrid_swa_conformer_pool_sgu"


def rms_norm(x, g):
    xf = x.astype(jnp.float32)
    y = xf * lax.rsqrt(jnp.mean(xf * xf, axis=-1, keepdims=True) + EPS)
    return (y * g.astype(jnp.float32)).astype(x.dtype)


def layer_norm(x, g, b):
    xf = x.astype(jnp.float32)
    mu = jnp.mean(xf, axis=-1, keepdims=True)
    var = jnp.mean(jnp.square(xf - mu), axis=-1, keepdims=True)
    y = (xf - mu) * lax.rsqrt(var + EPS)
    return (y * g.astype(jnp.float32) + b.astype(jnp.float32)).astype(x.dtype)


def sliding_window_attention(q, k, v, sinks):
    B, S = q.shape[0], q.shape[1]
    nb = S // ATTN_BLOCK
    G = N_Q_HEADS // N_KV_HEADS
    qb = q.reshape(B, nb, ATTN_BLOCK, N_KV_HEADS, G, HEAD_DIM)
    kb = k.reshape(B, nb, ATTN_BLOCK, N_KV_HEADS, HEAD_DIM)
    vb = v.reshape(B, nb, ATTN_BLOCK, N_KV_HEADS, HEAD_DIM)

    def with_prev(t):
        prev = jnp.pad(t, ((0, 0), (1, 0), (0, 0), (0, 0), (0, 0)))[:, :-1]
        return jnp.concatenate([prev, t], axis=2)

    kw, vw = with_prev(kb), with_prev(vb)
    logits = jnp.einsum('bnqkgd,bnskd->bnkgqs', qb, kw).astype(jnp.float32) * (HEAD_DIM ** -0.5)
    qi = jnp.arange(ATTN_BLOCK)[:, None]
    r = jnp.arange(2 * ATTN_BLOCK)[None, :]
    dist = qi + ATTN_BLOCK - r
    band = (dist >= 0) & (dist < WINDOW)
    key_pos = jnp.arange(nb)[:, None, None] * ATTN_BLOCK + r[None] - ATTN_BLOCK
    mask = band[None] & (key_pos >= 0)
    logits = jnp.where(mask[None, :, None, None], logits, -jnp.inf)
    sink = sinks.astype(jnp.float32).reshape(1, 1, N_KV_HEADS, G, 1, 1)
    m = jnp.maximum(jnp.max(logits, axis=-1, keepdims=True), sink)
    p = jnp.exp(logits - m)
    probs = p / (jnp.sum(p, axis=-1, keepdims=True) + jnp.exp(sink - m))
    out = jnp.einsum('bnkgqs,bnskd->bnqkgd', probs.astype(v.dtype), vw)
    return out.reshape(B, S, ATTN_WIDTH)


def conformer_conv(c, conv_w, conv_b, ln_g, ln_b):
    a, gate = jnp.split(c, 2, axis=-1)
    h = a * jax.nn.sigmoid(gate)
    h = lax.conv_general_dilated(
        h, conv_w[:, None, :].astype(h.dtype), window_strides=(1,),
        padding=[(CONV_KERNEL - 1, 0)],
        dimension_numbers=('NWC', 'WIO', 'NWC'),
        feature_group_count=CONV_WIDTH) + conv_b
    h = layer_norm(h, ln_g, ln_b)
    return jax.nn.silu(h)


def multiscale_pool(z, w_pool, scale):
    S = z.shape[1]
    zf = z.astype(jnp.float32)
    cs = jnp.cumsum(zf, axis=1)
    t = jnp.arange(S)
    outs = []
    for g, w in enumerate(POOL_WINDOWS):
        lo, hi = g * POOL_GROUP, (g + 1) * POOL_GROUP
        c = cs[..., lo:hi]
        prev = jnp.pad(c, ((0, 0), (w, 0), (0, 0)))[:, :S]
        cnt = jnp.minimum(t + 1, w).astype(jnp.float32)[:, None]
        pooled = (c - prev) / cnt - zf[..., lo:hi]
        outs.append(jnp.einsum('bsc,cd->bsd', pooled.astype(z.dtype), w_pool[g]))
    return jnp.concatenate(outs, axis=-1) * scale


def chunked_spatial_gating(z, ln_g, ln_b, w_s, b_s):
    B, S = z.shape[0], z.shape[1]
    u, v = jnp.split(jax.nn.gelu(z), 2, axis=-1)
    v = layer_norm(v, ln_g, ln_b)
    nc = S // SGU_CHUNK
    vc = v.reshape(B, nc, SGU_CHUNK, SGU_HEADS, SGU_HEAD_DIM)
    causal = jnp.tril(jnp.ones((SGU_CHUNK, SGU_CHUNK), dtype=bool))
    w = jnp.where(causal[None], w_s, jnp.zeros_like(w_s))
    mixed = jnp.einsum('gts,bcsgh->bctgh', w, vc) + b_s.T[None, None, :, :, None]
    return u * mixed.reshape(B, S, SGU_WIDTH)


def swiglu(h, w_gate, w_up, w_down):
    return (jax.nn.silu(h @ w_gate) * (h @ w_up)) @ w_down


def setup_inputs(seed: int = 0) -> dict:
    key = jax.random.key(seed)
    ks = jax.random.split(key, 24)
    f32 = jnp.float32

    def nrm(k, shape, s):
        return jax.random.normal(k, shape, f32) * s

    def gain(k, shape):
        return 1.0 + 0.02 * jax.random.normal(k, shape, f32)

    return {
        'x': jax.random.normal(ks[0], (BATCH, SEQ, D_MODEL), f32),
        'mix_norm': gain(ks[1], (DEPTH, D_MODEL)),
        'a_w_in': nrm(ks[2], (N_EVEN, D_MODEL, IN0_WIDTH), D_MODEL ** -0.5),
        'a_b_in': nrm(ks[3], (N_EVEN, IN0_WIDTH), 0.02),
        'a_sinks': nrm(ks[4], (N_EVEN, N_Q_HEADS), 1.0),
        'a_conv_w': nrm(ks[5], (N_EVEN, CONV_KERNEL, CONV_WIDTH), CONV_KERNEL ** -0.5),
        'a_conv_b': nrm(ks[6], (N_EVEN, CONV_WIDTH), 0.02),
        'a_cln_g': gain(ks[7], (N_EVEN, CONV_WIDTH)),
        'a_cln_b': nrm(ks[8], (N_EVEN, CONV_WIDTH), 0.02),
        'a_w_out': nrm(ks[9], (N_EVEN, D_MODEL, D_MODEL), D_MODEL ** -0.5),
        'c_w_in': nrm(ks[10], (N_ODD, D_MODEL, IN1_WIDTH), D_MODEL ** -0.5),
        'c_w_pool': nrm(ks[11], (N_ODD, len(POOL_WINDOWS), POOL_GROUP, POOL_GROUP), POOL_GROUP ** -0.5),
        'c_pool_scale': gain(ks[12], (N_ODD, POOL_WIDTH)),
        'c_sln_g': gain(ks[13], (N_ODD, SGU_WIDTH)),
        'c_sln_b': nrm(ks[14], (N_ODD, SGU_WIDTH), 0.02),
        'c_w_s': nrm(ks[15], (N_ODD, SGU_HEADS, SGU_CHUNK, SGU_CHUNK), SGU_CHUNK ** -0.5),
        'c_b_s': gain(ks[16], (N_ODD, SGU_HEADS, SGU_CHUNK)),
        'c_w_out': nrm(ks[17], (N_ODD, D_MODEL, D_MODEL), D_MODEL ** -0.5),
        'ffn_norm': gain(ks[18], (DEPTH, D_MODEL)),
        'ffn_w_gate': nrm(ks[19], (DEPTH, D_MODEL, D_FF), D_MODEL ** -0.5),
        'ffn_w_up': nrm(ks[20], (DEPTH, D_MODEL, D_FF), D_MODEL ** -0.5),
        'ffn_w_down': nrm(ks[21], (DEPTH, D_FF, D_MODEL), D_FF ** -0.5),
        'final_norm': gain(ks[22], (D_MODEL,)),
    }


def reference(x, mix_norm, a_w_in, a_b_in, a_sinks, a_conv_w, a_conv_b, a_cln_g, a_cln_b, a_w_out,
              c_w_in, c_w_pool, c_pool_scale, c_sln_g, c_sln_b, c_w_s, c_b_s, c_w_out,
              ffn_norm, ffn_w_gate, ffn_w_up, ffn_w_down, final_norm):
    B, S = x.shape[0], x.shape[1]
    h = x
    for i in range(DEPTH):
        j = i // 2
        hn = rms_norm(h, mix_norm[i])
        if i % 2 == 0:
            z = hn @ a_w_in[j] + a_b_in[j]
            q = z[..., :ATTN_WIDTH].reshape(B, S, N_Q_HEADS, HEAD_DIM)
            k = z[..., ATTN_WIDTH:ATTN_WIDTH + KV_WIDTH].reshape(B, S, N_KV_HEADS, HEAD_DIM)
            v = z[..., ATTN_WIDTH + KV_WIDTH:ATTN_WIDTH + 2 * KV_WIDTH].reshape(B, S, N_KV_HEADS, HEAD_DIM)
            c = z[..., ATTN_WIDTH + 2 * KV_WIDTH:]
            attn = sliding_window_attention(q, k, v, a_sinks[j])
            conv = conformer_conv(c, a_conv_w[j], a_conv_b[j], a_cln_g[j], a_cln_b[j])
            h = h + jnp.concatenate([attn, conv], axis=-1) @ a_w_out[j]
        else:
            z = hn @ c_w_in[j]
            pool = multiscale_pool(z[..., :POOL_WIDTH], c_w_pool[j], c_pool_scale[j])
            sgu = chunked_spatial_gating(z[..., POOL_WIDTH:], c_sln_g[j], c_sln_b[j], c_w_s[j], c_b_s[j])
            h = h + jnp.concatenate([pool, sgu], axis=-1) @ c_w_out[j]
        h = h + swiglu(rms_norm(h, ffn_norm[i]), ffn_w_gate[i], ffn_w_up[i], ffn_w_down[i])
    return rms_norm(h, final_norm)
```

```python
import numpy as np
import concourse.bass as bass
import concourse.mybir as mybir
from concourse.bass_utils import run_bass_kernel_spmd

F32 = mybir.dt.float32
BF = mybir.dt.bfloat16
AF = mybir.ActivationFunctionType
ALU = mybir.AluOpType
AX = mybir.AxisListType

NCORES = 8
D = 1024
KD = 8
DFF = 2816
NF = 22
ST = 512
NB = 4
EPS = 1e-5
SLAB = 4096
NSLAB = 49
RING = 4
NDSEM = 8

C_GMIX0, C_GMIX1, C_GFFN0, C_GFFN1, C_GFIN = 0, 8, 16, 24, 32
C_BIN = 40
C_CONVW = 54
C_CONVB = 178
C_CLNG = 182
C_CLNB = 186
C_PSCALE = 190
C_SINK = 194
NCOL = 202
K_IDENT = 0
K_MASK = 128
K_TRIL = 640
K_POOL = 768
NCST = 768 + 1536
R_BV = 0
R_BS = 128
NROW = 640


def _esize(dt):
    return 2 if dt == BF else 4


class Op:
    __slots__ = ("id", "stream", "emit", "deps", "sig", "cnt", "dma", "dsem", "dval", "label")


class Prog:
    STREAMS = ("pe", "act", "dve", "pool", "sp")

    def __init__(self):
        self.nc = bass.Bass("TRN2", target_bir_lowering=False)
        self.ops = []
        self.streams = {s: [] for s in self.STREAMS}
        self.track = {}
        self.ndma = {s: 0 for s in self.STREAMS}
        self.dma_ops = {s: [] for s in self.STREAMS}
        self.out_dmas = []
        self.label = ""

    @staticmethod
    def _rng(ap):
        t = ap.tensor
        es = _esize(ap.dtype)
        dims = ap.ap
        space = str(ap.space)
        if "DRAM" in space.upper() or "HBM" in space.upper():
            lo = ap.offset
            ext = 0
            for (stp, cnt) in dims:
                ext += (cnt - 1) * abs(stp)
            return (t.name, lo * es, (lo + ext + 1) * es, 0, 128)
        pstep, pcnt = dims[0]
        p0 = ap.start_partition()
        lo = ap.offset - p0 * pstep
        ext = 0
        for (stp, cnt) in dims[1:]:
            ext += (cnt - 1) * abs(stp)
        if "PSUM" in space.upper():
            b0 = (lo * es) // 2048
            b1 = ((lo + ext + 1) * es + 2047) // 2048
            return ("~psum", b0 * 2048, b1 * 2048, 0, 128)
        return (t.name, lo * es, (lo + ext + 1) * es, p0, p0 + pcnt)

    def add(self, stream, emit, reads=(), writes=(), dma=False, is_out=False):
        op = Op()
        op.id = len(self.ops)
        op.stream = stream
        op.emit = emit
        op.deps = set()
        op.sig = False
        op.cnt = 0
        op.dma = dma
        op.dsem = None
        op.dval = 0
        op.label = self.label
        if dma:
            n = self.ndma[stream]
            self.ndma[stream] = n + 1
            op.dsem = n % NDSEM
            op.dval = 16 * (n // NDSEM + 1)
            if n >= NDSEM:
                op.deps.add(self.dma_ops[stream][n - NDSEM].id)
            self.dma_ops[stream].append(op)
            if is_out:
                self.out_dmas.append(op)
        rr = [self._rng(a) for a in reads if a is not None]
        ww = [self._rng(a) for a in writes if a is not None]
        for (name, lo, hi, p0, p1) in rr:
            excl = name == "~psum"
            for rec in self.track.get(name, ()):
                if (rec[5] or (excl and rec[6][0] != stream)) and rec[0] < hi and lo < rec[1] \
                        and rec[2] < p1 and p0 < rec[3]:
                    op.deps.add(rec[4])
        for (name, lo, hi, p0, p1) in ww:
            for rec in self.track.get(name, ()):
                if rec[0] < hi and lo < rec[1] and rec[2] < p1 and p0 < rec[3]:
                    op.deps.add(rec[4])
        op.deps.discard(op.id)
        ekey = (stream, dma and op.id)
        for (name, lo, hi, p0, p1) in ww:
            lst = self.track.setdefault(name, [])
            lst[:] = [r for r in lst if not (lo <= r[0] and r[1] <= hi and p0 <= r[2] and r[3] <= p1)]
            lst.append((lo, hi, p0, p1, op.id, True, ekey))
        for (name, lo, hi, p0, p1) in rr:
            lst = self.track.setdefault(name, [])
            if not dma:
                lst[:] = [r for r in lst if not ((not r[5]) and r[6] == ekey and lo <= r[0] and r[1] <= hi
                                                 and p0 <= r[2] and r[3] <= p1)]
            lst.append((lo, hi, p0, p1, op.id, False, ekey))
        self.ops.append(op)
        self.streams[stream].append(op)
        return op

    def emit_all(self, block, sems, dsems):
        ops = self.ops
        for op in ops:
            for d in op.deps:
                dop = ops[d]
                if not dop.dma:
                    if dop.stream == "pe" and op.stream == "pe" and not op.dma:
                        continue
                    dop.sig = True
        for s in self.STREAMS:
            c = 0
            for op in self.streams[s]:
                if (not op.dma) and op.sig:
                    c += 1
                    op.cnt = c
        nwaits = {s: 0 for s in self.STREAMS}

        def run_stream(s, eng):
            seen = {}
            for op in self.streams[s]:
                waits = {}
                for d in op.deps:
                    dop = ops[d]
                    if dop.dma:
                        key = ("d", dop.stream, dop.dsem)
                        val = dop.dval
                    else:
                        if dop.stream == "pe" and s == "pe" and not op.dma:
                            continue
                        key = ("c", dop.stream)
                        val = dop.cnt
                    if waits.get(key, 0) < val:
                        waits[key] = val
                for key, val in waits.items():
                    if seen.get(key, 0) >= val:
                        continue
                    seen[key] = val
                    sem = dsems[key[1]][key[2]] if key[0] == "d" else sems[key[1]]
                    eng.wait_ge(sem, val)
                    nwaits[s] += 1
                ins = op.emit(eng)
                if op.dma:
                    ins.then_inc(dsems[s][op.dsem], 16)
                elif op.sig:
                    ins.then_inc(sems[s], 1)
            if s == "sp":
                fin = {}
                for op in self.out_dmas:
                    key = (op.stream, op.dsem)
                    fin[key] = max(fin.get(key, 0), op.dval)
                for (st_, di), val in fin.items():
                    eng.wait_ge(dsems[st_][di], val)

        @block.tensor
        def _(e):
            run_stream("pe", e)

        @block.scalar
        def _(e):
            run_stream("act", e)

        @block.vector
        def _(e):
            run_stream("dve", e)

        @block.gpsimd
        def _(e):
            run_stream("pool", e)

        @block.sync
        def _(e):
            run_stream("sp", e)

        self.nwaits = nwaits


def perm_q():
    idx = np.zeros(512, dtype=np.int64)
    for c in range(4):
        for half in range(2):
            for d in range(64):
                idx[c * 128 + half * 64 + d] = (half * 4 + c) * 64 + d
    return idx


def perm_in0():
    pq = perm_q()
    cols = list(pq) + list(range(512, 768))
    for cc in range(4):
        cols += list(range(768 + cc * 128, 768 + (cc + 1) * 128))
        cols += list(range(1280 + cc * 128, 1280 + (cc + 1) * 128))
    return np.array(cols, dtype=np.int64)


def host_consts():
    cst = np.zeros((128, NCST), dtype=np.float32)
    cst[:, K_IDENT:K_IDENT + 128] = np.eye(128, dtype=np.float32)
    qi = np.arange(128)[:, None]
    r = np.arange(256)[None, :]
    dist = qi + 128 - r
    band = (dist >= 0) & (dist < 128)
    NEG = -30000.0
    cst[:, K_MASK:K_MASK + 256] = np.where(band, 0.0, NEG)
    first = band & (r >= 128)
    cst[:, K_MASK + 256:K_MASK + 512] = np.where(first, 0.0, NEG)
    s = np.arange(128)[:, None]
    t = np.arange(128)[None, :]
    cst[:, K_TRIL:K_TRIL + 128] = (s <= t).astype(np.float32)
    for g, w in enumerate((2, 4, 8, 16)):
        cur = np.where((t - s >= 0) & (t - s < w), 1.0 / w, 0.0) - (s == t)
        prev = np.where((t + 128 - s) < w, 1.0 / w, 0.0)
        cnt = np.minimum(t + 1, w).astype(np.float64)
        fst = np.where((t - s >= 0) & (t - s < w), 1.0 / cnt, 0.0) - (s == t)
        cst[:, K_POOL + g * 128:K_POOL + (g + 1) * 128] = cur
        cst[:, K_POOL + (4 + g) * 128:K_POOL + (5 + g) * 128] = prev
        cst[:, K_POOL + (8 + g) * 128:K_POOL + (9 + g) * 128] = fst
    return cst


def prep_inputs(inp, ncores, nseq):
    f = lambda a: np.ascontiguousarray(np.asarray(a, dtype=np.float32))
    x = f(inp["x"])
    pin = perm_in0()
    pq = perm_q()
    a_w_in = f(inp["a_w_in"])[0][:, pin]
    a_b_in = f(inp["a_b_in"])[0][pin]
    a_w_out = f(inp["a_w_out"])[0]
    a_w_out = np.concatenate([a_w_out[pq], a_w_out[512:]], axis=0)
    cols = np.zeros((128, NCOL), dtype=np.float32)
    colv = lambda v: np.asarray(v, dtype=np.float32).reshape(-1, 128).T
    cols[:, C_GMIX0:C_GMIX0 + 8] = colv(inp["mix_norm"][0])
    cols[:, C_GMIX1:C_GMIX1 + 8] = colv(inp["mix_norm"][1])
    cols[:, C_GFFN0:C_GFFN0 + 8] = colv(inp["ffn_norm"][0])
    cols[:, C_GFFN1:C_GFFN1 + 8] = colv(inp["ffn_norm"][1])
    cols[:, C_GFIN:C_GFIN + 8] = colv(inp["final_norm"])
    cols[:, C_BIN:C_BIN + 14] = colv(a_b_in)
    cw = f(inp["a_conv_w"])[0]
    cols[:, C_CONVW:C_CONVW + 124] = cw.T.reshape(4, 128, 31).transpose(1, 0, 2).reshape(128, 124)
    cols[:, C_CONVB:C_CONVB + 4] = colv(inp["a_conv_b"][0])
    cols[:, C_CLNG:C_CLNG + 4] = colv(inp["a_cln_g"][0])
    cols[:, C_CLNB:C_CLNB + 4] = colv(inp["a_cln_b"][0])
    cols[:, C_PSCALE:C_PSCALE + 4] = colv(inp["c_pool_scale"][0])
    cols[:, C_SINK:C_SINK + 8] = np.broadcast_to(f(inp["a_sinks"])[0][None, :], (128, 8))
    rows = np.zeros((1, NROW), dtype=np.float32)
    rows[0, R_BV:R_BV + 128] = a_b_in[640:768]
    rows[0, R_BS:R_BS + 512] = f(inp["c_b_s"])[0].reshape(512)
    bc = np.zeros((128, 2, 512), dtype=np.float32)
    bc[:, 0, :] = np.broadcast_to(f(inp["c_sln_g"])[0][None, :], (128, 512))
    bc[:, 1, :] = np.broadcast_to(f(inp["c_sln_b"])[0][None, :], (128, 512))
    wpool = np.ascontiguousarray(f(inp["c_w_pool"])[0].transpose(1, 0, 2))
    wst = np.ascontiguousarray(f(inp["c_w_s"])[0].transpose(2, 0, 1))
    shared = {
        "w_in0": np.ascontiguousarray(a_w_in), "w_out0": np.ascontiguousarray(a_w_out),
        "w_in1": f(inp["c_w_in"])[0], "w_out1": f(inp["c_w_out"])[0],
        "wg": f(inp["ffn_w_gate"]), "wu": f(inp["ffn_w_up"]), "wd": f(inp["ffn_w_down"]),
        "cols": cols, "rows": rows, "bc": bc, "wpool": wpool, "wst": wst, "cst": host_consts(),
    }
    maps = []
    for c in range(ncores):
        m = dict(shared)
        m["x"] = np.ascontiguousarray(x[c * nseq:(c + 1) * nseq])
        maps.append(m)
    return maps


def build(nseq, seqlen, dbg=None):
    pg = Prog()
    nc = pg.nc
    nst_seq = seqlen // ST
    ntiles = nseq * nst_seq
    dram_in = lambda name, shape: nc.dram_tensor(name, shape, F32, kind="ExternalInput").ap()
    x = dram_in("x", [nseq, seqlen, D])
    w_in0 = dram_in("w_in0", [D, 1792])
    w_out0 = dram_in("w_out0", [D, D])
    w_in1 = dram_in("w_in1", [D, 1536])
    w_out1 = dram_in("w_out1", [D, D])
    wg = dram_in("wg", [2, D, DFF])
    wu = dram_in("wu", [2, D, DFF])
    wd = dram_in("wd", [2, DFF, D])
    cols_d = dram_in("cols", [128, NCOL])
    rows_d = dram_in("rows", [1, NROW])
    bc_d = dram_in("bc", [128, 2, 512])
    wpool_d = dram_in("wpool", [128, 4, 128])
    wst_d = dram_in("wst", [128, 4, 128])
    cst_d = dram_in("cst", [128, NCST])
    out = nc.dram_tensor("out", [nseq, seqlen, D], F32, kind="ExternalOutput").ap()
    wscr = nc.dram_tensor("wscr", [NSLAB, 128, SLAB], BF).ap()
    dbg_out = {}
    if dbg:
        for name, shape in dbg.items():
            dbg_out[name] = nc.dram_tensor("dbg_" + name, shape, F32, kind="ExternalOutput").ap()

    import contextlib
    es = contextlib.ExitStack()
    with es:
        sb = lambda name, shape, dt: es.enter_context(nc.sbuf_tensor(name, shape, dt))
        cols = sb("cols_s", [128, NCOL], F32)
        nsink = sb("nsink", [128, 8], F32)
        ident = sb("ident", [128, 128], F32)
        identb = sb("identb", [128, 128], BF)
        mask2 = sb("mask2", [128, 2, 256], F32)
        ones_b = sb("ones_b", [128, 128], BF)
        onesm_b = sb("onesm_b", [128, 128], BF)
        rows_b = sb("rows_b", [1, NROW], BF)
        bcs = sb("bcs", [128, 2, 512], F32)
        poolm = sb("poolm", [128, 12, 128], BF)
        wpool_b = sb("wpool_b", [128, 4, 128], BF)
        wst_b = sb("wst_b", [128, 4, 128], BF)
        NDIAG = 64
        diagbuf = sb("diagbuf", [128, NDIAG, 128], BF)
        col_eps = sb("col_eps", [128, 1], F32)
        ring = sb("ring", [128, RING, SLAB], BF)
        h = [sb("h%d" % i, [128, KD, ST], F32) for i in range(2)]
        hn = [sb("hn%d" % i, [128, KD, ST], BF) for i in range(2)]
        stg_in = [sb("stgi%d" % i, [128, D], F32) for i in range(4)]
        stg_out = [sb("stgo%d" % i, [128, D], F32) for i in range(2)]
        kT = sb("kT", [128, 128 + ST], BF)
        vtok = sb("vtok", [128, 1 + NB, 128], BF)
        glu = sb("glu", [128, 4, 30 + ST], BF)
        zptok = sb("zptok", [128, 1 + NB, 512], BF)
        qT = sb("qT", [128, 4, ST], BF)
        cat = sb("cat", [128, KD, ST], BF)
        arena = sb("arena", [128, 24 * 1024], BF)
        psum = es.enter_context(nc.psum_tensor("psum", [128, 8, 512], F32))
        sems = {s: es.enter_context(nc.semaphore("sem_" + s)) for s in Prog.STREAMS}
        dsems = {s: [es.enter_context(nc.semaphore("dsem_%s%d" % (s, i))) for i in range(NDSEM)]
                 for s in ("sp", "pool", "act")}
        block = es.enter_context(nc.Block())

        def aview(off_b, nelem, dt):
            o = off_b // 2
            n = nelem * _esize(dt) // 2
            v = arena[:, o:o + n]
            return v if dt == BF else v.bitcast(dt)

        KB = 1024
        sqrot_a = [aview(i * KB, 512, BF) for i in range(2)]
        sqblk = [aview(32 * KB + i * 2 * KB, 1024, BF).rearrange("p (k t) -> p k t", t=128) for i in range(2)]
        small = aview(6 * KB, 256, F32)
        rstd_s = aview(30 * KB, 512, F32)
        sig = [aview(10 * KB + i * 2 * KB, 512, F32) for i in range(2)]
        Sm = [aview(14 * KB + i * 4 * KB, 1024, F32).rearrange("p (g s) -> p g s", s=256) for i in range(2)]
        Osb = aview(44 * KB, 512, BF)
        st8 = aview(45 * KB, 256, F32)
        Pb = [aview(22 * KB + i * 2 * KB, 1024, BF).rearrange("p (g s) -> p g s", s=256) for i in range(2)]
        PT = [aview(o_ * KB, 1024, BF).rearrange("p (a c) -> p a c", c=512) for o_ in (26, 46)]
        cv = aview(28 * KB, 2048, F32).rearrange("p (c t) -> p c t", t=ST)
        cvb = aview(36 * KB, 2048, BF).rearrange("p (c t) -> p c t", t=ST)
        cvq = aview(40 * KB, 2048, BF).rearrange("p (c t) -> p c t", t=ST)
        lnt = [aview(i * 2 * KB, 512, F32) for i in range(3)]
        actb = aview(0, NF * ST, BF).rearrange("p (f t) -> p f t", t=ST)
        sgt = [aview(22 * KB + i * KB, 512, BF) for i in range(4)]
        sqrot_f = [aview(26 * KB + i * KB, 512, BF) for i in range(2)]
        uT = aview(0, 2048, BF).rearrange("p (c t) -> p c t", t=ST)
        vg = [aview(4 * KB + i * 2 * KB, 512, F32) for i in range(2)]
        vn = [aview(8 * KB + i * 2 * KB, 512, F32) for i in range(2)]
        vln = aview(12 * KB, 2048, BF).rearrange("p (b c) -> p b c", c=512)
        pooled = aview(16 * KB, 2048, BF).rearrange("p (c t) -> p c t", t=ST)
        sqrot_c = [aview(20 * KB + i * KB, 512, BF) for i in range(2)]
        smallc = aview(28 * KB, 256, F32)
        pstg = [aview(i * 16 * KB, SLAB, F32) for i in range(2)]
        pbf = [aview(32 * KB + i * 8 * KB, SLAB, BF) for i in range(2)]
        cstg = aview(8 * KB, 1536, F32)

        def mm(o, lhsT, rhs, start, stop):
            op = pg.add("pe", lambda e: e.matmul(o, lhsT=lhsT, rhs=rhs, start=start, stop=stop),
                        reads=[lhsT, rhs], writes=[o])
            op.label = (op.label, "MATMUL %d*%d*%d" % (lhsT.shape[0], lhsT.shape[1], rhs.shape[-1]))

        def tr(o, in_, idn):
            op = pg.add("pe", lambda e: e.transpose(o, in_, idn), reads=[in_, idn], writes=[o])
            op.label = (op.label, "TR")

        def act(o, in_, func, bias=None, scale=None, accum=None):
            kw = {}
            rd = [in_]
            if bias is not None:
                kw["bias"] = bias
                if not isinstance(bias, float):
                    rd.append(bias)
            if scale is not None:
                kw["scale"] = scale
                if not isinstance(scale, float):
                    rd.append(scale)
            wr = [o]
            if accum is not None:
                kw["accum_out"] = accum
                wr.append(accum)
            pg.add("act", lambda e: e.activation(out=o, in_=in_, func=func, **kw), reads=rd, writes=wr)

        def tt(o, a, b, op, eng="dve"):
            pg.add(eng, lambda e: e.tensor_tensor(out=o, in0=a, in1=b, op=op), reads=[a, b], writes=[o])

        def ts(o, a, s1, op0, s2=None, op1=None, eng="dve"):
            rd = [a]
            if not isinstance(s1, float):
                rd.append(s1)
            if s2 is not None and not isinstance(s2, float):
                rd.append(s2)
            if op1 is None:
                pg.add(eng, lambda e: e.tensor_scalar(out=o, in0=a, scalar1=s1, scalar2=None, op0=op0),
                       reads=rd, writes=[o])
            else:
                pg.add(eng, lambda e: e.tensor_scalar(out=o, in0=a, scalar1=s1, scalar2=s2, op0=op0, op1=op1),
                       reads=rd, writes=[o])

        def stt(o, a, s, b, op0, op1):
            rd = [a, b]
            if not isinstance(s, float):
                rd.append(s)
            pg.add("dve", lambda e: e.scalar_tensor_tensor(out=o, in0=a, scalar=s, in1=b, op0=op0, op1=op1),
                   reads=rd, writes=[o])

        def cp(o, a, eng="dve"):
            if eng == "act":
                pg.add("act", lambda e: e.activation(out=o, in_=a, func=AF.Copy), reads=[a], writes=[o])
            else:
                pg.add(eng, lambda e: e.tensor_copy(out=o, in_=a), reads=[a], writes=[o])

        def memset(o, val, eng="dve"):
            pg.add(eng, lambda e: e.memset(o, val), reads=[], writes=[o])

        def recip(o, a):
            pg.add("dve", lambda e: e.reciprocal(out=o, in_=a), reads=[a], writes=[o])

        def reduce_max(o, a):
            pg.add("dve", lambda e: e.tensor_reduce(out=o, in_=a, axis=AX.X, op=ALU.max), reads=[a], writes=[o])

        def dma(o, in_, q="sp", rd_track=True, is_out=False):
            reads = [in_] if rd_track else []
            if q == "sp":
                pg.add("sp", lambda e: e.dma_start(out=o, in_=in_), reads=reads, writes=[o], dma=True, is_out=is_out)
            elif q == "pool":
                pg.add("pool", lambda e: e.dma_start(out=o, in_=in_), reads=reads, writes=[o], dma=True, is_out=is_out)
            else:
                pg.add("act", lambda e: e.dma_start(out=o, in_=in_), reads=reads, writes=[o], dma=True, is_out=is_out)

        def col(c):
            return cols[:, c:c + 1]

        bank = lambda b: psum[:, b, :]
        _rr = {"i": 0}

        def nbank(pool):
            _rr["i"] += 1
            return pool[_rr["i"] % len(pool)]

        POOL_A = [0, 1, 2]
        POOL_F = [0, 1, 2, 3, 4, 5, 6]
        B_S0, B_PT, B_O, B_ST = 3, 5, 6, 7

        dma(cols[:], cols_d, rd_track=False)
        dma(bcs[:], bc_d, rd_track=False)
        dma(cstg[:, 0:640], cst_d[:, 0:640], rd_track=False)
        cp(ident[:], cstg[:, K_IDENT:K_IDENT + 128])
        cp(identb[:], cstg[:, K_IDENT:K_IDENT + 128])
        cp(mask2[:], cstg[:, K_MASK:K_MASK + 512].rearrange("p (a s) -> p a s", s=256))
        memset(ones_b[:], 1.0)
        memset(onesm_b[:], 1.0 / 512.0)
        ts(nsink[:], cols[:, C_SINK:C_SINK + 8], -1.0, ALU.mult)
        trilf = aview(14 * KB, 128, F32)
        wsf = aview(0, 512, F32).rearrange("p (g t) -> p g t", t=128)
        dma(trilf, cst_d[:, K_TRIL:K_TRIL + 128], rd_track=False)
        dma(wsf, wst_d, rd_track=False)
        for g in range(4):
            tt(wst_b[:, g, :], wsf[:, g, :], trilf, ALU.mult)
        wpf = aview(2 * KB, 512, F32).rearrange("p (g t) -> p g t", t=128)
        dma(wpf, wpool_d, rd_track=False)
        cp(wpool_b[:], wpf)
        rowf = aview(4 * KB, NROW, F32)[0:1, :]
        dma(rowf, rows_d, rd_track=False)
        cp(rows_b[:], rowf)
        dma(cstg[:], cst_d[:, K_POOL:K_POOL + 1536], rd_track=False)
        cp(poolm[:], cstg[:].rearrange("p (m t) -> p m t", t=128))
        memset(col_eps[:], EPS)
        memset(kT[:, 0:128], 0.0)
        memset(vtok[:, 0, :], 0.0)
        memset(zptok[:, 0, :], 0.0)
        dstate = {"i": 0}

        slab_no = {"i": 0}

        slab_src = {}

        def conv_slab(srcs, width):
            j = slab_no["i"]
            slab_no["i"] += 1
            slab_src[j] = (srcs, width)
            return j

        slabs = {}
        w_in0_v = w_in0.rearrange("(k p) f -> p k f", p=128)
        w_out0_v = w_out0.rearrange("(k p) f -> p k f", p=128)
        w_in1_v = w_in1.rearrange("(k p) f -> p k f", p=128)
        w_out1_v = w_out1.rearrange("(k p) f -> p k f", p=128)
        for j in range(4):
            nco = 512 if j < 3 else 256
            slabs[("in0", j)] = conv_slab([(8, 0, nco, w_in0_v[:, :, j * 512:j * 512 + nco])], nco)
        for j in range(2):
            slabs[("out0", j)] = conv_slab([(8, 0, 512, w_out0_v[:, :, j * 512:(j + 1) * 512])], 512)

        def ffn_slabs(l):
            wg_v = wg[l].rearrange("(k p) f -> p k f", p=128)
            wu_v = wu[l].rearrange("(k p) f -> p k f", p=128)
            wd_v = wd[l].rearrange("(k p) f -> p k f", p=128)
            for j in range(11):
                srcs = []
                for i in range(2):
                    f_ = 2 * j + i
                    srcs.append((8, i * 256, 128, wg_v[:, :, f_ * 128:(f_ + 1) * 128]))
                    srcs.append((8, i * 256 + 128, 128, wu_v[:, :, f_ * 128:(f_ + 1) * 128]))
                slabs[("gu%d" % l, j)] = conv_slab(srcs, 512)
            for j in range(8):
                slabs[("dn%d" % l, j)] = conv_slab([(NF, 0, 128, wd_v[:, :, j * 128:(j + 1) * 128])], 128)


        ffn_slabs(0)
        for j, c0 in enumerate((0, 1024, 512)):
            slabs[("in1", j)] = conv_slab([(8, 0, 512, w_in1_v[:, :, c0:c0 + 512])], 512)
        for j in range(2):
            slabs[("out1", j)] = conv_slab([(8, 0, 512, w_out1_v[:, :, j * 512:(j + 1) * 512])], 512)
        ffn_slabs(1)
        assert slab_no["i"] == NSLAB
        slab_elems = {}
        for (nm, j), idx in slabs.items():
            if nm.startswith("in0"):
                slab_elems[idx] = 8 * (512 if j < 3 else 256)
            elif nm.startswith("dn"):
                slab_elems[idx] = NF * 128
            else:
                slab_elems[idx] = 8 * 512
        order = [("in0", j) for j in range(4)] + [("out0", j) for j in range(2)] + \
                [("gu0", j) for j in range(11)] + [("dn0", j) for j in range(8)] + \
                [("in1", j) for j in range(3)] + [("out1", j) for j in range(2)] + \
                [("gu1", j) for j in range(11)] + [("dn1", j) for j in range(8)]
        seq_slabs = [slabs[k] for k in order]
        total_uses = ntiles * NSLAB
        wstate = {"next_load": NSLAB, "next_use": 0, "next_conv": 0}

        def issue_load():
            q = wstate["next_load"]
            if q >= total_uses:
                return
            wstate["next_load"] = q + 1
            idx = seq_slabs[q % NSLAB]
            n = slab_elems[idx]
            dma(ring[:, q % RING, 0:n], wscr[idx, :, 0:n], q="sp")

        def issue_conv():
            q = wstate["next_conv"]
            wstate["next_conv"] = q + 1
            idx = seq_slabs[q]
            srcs, width = slab_src[idx]
            kk = srcs[0][0]
            dst = ring[:, q % 2, :]
            k0 = 0
            for hf in range(2):
                nk = (kk + 1) // 2 if hf == 0 else kk - (kk + 1) // 2
                stg = ring[:, 2 + hf, :].bitcast(F32)
                for (kk_, c0, ncol, src) in srcs:
                    d_ = stg[:, 0:nk * width].rearrange("p (k c) -> p k c", c=width)[:, :, c0:c0 + ncol]
                    dma(d_, src[:, k0:k0 + nk, :], q="sp", rd_track=False)
                cp(dst[:, k0 * width:(k0 + nk) * width], stg[:, 0:nk * width], eng="pool")
                k0 += nk
            n = slab_elems[idx]
            dma(wscr[idx, :, 0:n], dst[:, 0:n], q="pool")

        def use_slab(key):
            q = wstate["next_use"]
            wstate["next_use"] = q + 1
            assert seq_slabs[q % NSLAB] == slabs[key], (key, q)
            if q < NSLAB:
                while wstate["next_conv"] <= min(q + 1, NSLAB - 1):
                    issue_conv()
                if q >= NSLAB - 1:
                    while wstate["next_load"] < min(q + RING, total_uses):
                        issue_load()
                return ring[:, q % 2, :]
            while wstate["next_load"] < min(q + RING, total_uses):
                issue_load()
            return ring[:, q % RING, :]

        def load_x(ti):
            s_, st_ = divmod(ti, nst_seq)
            for b in range(NB):
                t0 = st_ * ST + b * 128
                dma(stg_in[b][:], x[s_, t0:t0 + 128, :], q="pool", rd_track=False)

        class StatAcc:
            def __init__(self, bufs):
                self.bufs = bufs
                self.n = 0
                self.pend = None

            def _flush(self, last):
                if self.pend is not None:
                    i, b = self.pend
                    mm(bank(B_ST), ones_b[:], b, i == 0, last)
                    self.pend = None

            def add(self, src):
                b = self.bufs[self.n % 2]
                act(b, src, AF.Square)
                self._flush(False)
                self.pend = (self.n, b)
                self.n += 1

            def finish(self):
                assert self.n == KD
                self._flush(True)

        def norm_finish(hh, gcol0, hno, bst=None):
            bst = B_ST if bst is None else bst
            act(rstd_s, bank(bst), AF.Sqrt, bias=col_eps[:], scale=1.0 / D)
            recip(bank(bst), rstd_s)
            for k in range(KD):
                stt(hno[:, k, :], hh[:, k, :], col(gcol0 + k), bank(bst), ALU.mult, ALU.mult)

        def proj_resid(hh, src, slab_keys, sqbufs):
            sa = StatAcc(sqbufs)
            for j, key in enumerate(slab_keys):
                w = use_slab(key).rearrange("p (k c) -> p k c", c=512)
                for cc in range(4):
                    c = j * 4 + cc
                    bk = nbank(POOL_A)
                    for k in range(KD):
                        mm(bank(bk), w[:, k, cc * 128:(cc + 1) * 128], src[:, k, :], k == 0, k == KD - 1)
                    tt(hh[:, c, :], bank(bk), hh[:, c, :], ALU.add)
                    sa.add(hh[:, c, :])
            sa.finish()

        def ffn_gu(hh, l, hnb, hook=None, pre_hook=None):
            pg.label = "F%d_norm" % l
            norm_finish(hh, C_GFFN0 if l == 0 else C_GFFN1, hnb)
            if pre_hook is not None:
                pre_hook()
            pg.label = "F%d_gu" % l
            for j in range(11):
                w = use_slab(("gu%d" % l, j)).rearrange("p (k c) -> p k c", c=512)
                for i in range(2):
                    f_ = 2 * j + i
                    bg = nbank(POOL_F)
                    for k in range(KD):
                        mm(bank(bg), w[:, k, i * 256:i * 256 + 128], hnb[:, k, :], k == 0, k == KD - 1)
                    bu = nbank(POOL_F)
                    for k in range(KD):
                        mm(bank(bu), w[:, k, i * 256 + 128:i * 256 + 256], hnb[:, k, :], k == 0, k == KD - 1)
                    sg_ = sgt[f_ % 4]
                    act(sg_, bank(bg), AF.Silu)
                    tt(actb[:, f_, :], bank(bu), sg_, ALU.mult)
                if hook is not None and j == 7:
                    hook()
                    pg.label = "F%d_gu" % l

        def ffn_dn(hh, l):
            pg.label = "F%d_dn" % l
            sa = StatAcc(sqrot_f)
            for j in range(8):
                w = use_slab(("dn%d" % l, j)).rearrange("p (f c) -> p f c", c=128)
                bk = nbank(POOL_F)
                for f_ in range(NF):
                    mm(bank(bk), w[:, f_, :], actb[:, f_, :], f_ == 0, f_ == NF - 1)
                tt(hh[:, j, :], bank(bk), hh[:, j, :], ALU.add)
                sa.add(hh[:, j, :])
            sa.finish()

        def gen_diag(cc):
            slots = []
            for j in range(31):
                dslot = dstate["i"] % NDIAG
                dstate["i"] += 1
                ts(diagbuf[:, dslot, :], identb[:], col(C_CONVW + cc * 31 + j), ALU.mult, 0.0, ALU.add,
                   eng="pool")
                slots.append(dslot)
            return slots

        def A_pre(ti, hh, hnb, bpool):
            pg.label = "A1_xT"
            pend = None

            def blk_stats(b):
                for k in range(KD):
                    mm(psum[:, B_O, b * 128:(b + 1) * 128], ones_b[:], sqblk[b % 2][:, k, :], k == 0, k == KD - 1)

            for b in range(NB):
                if b >= 2:
                    blk_stats(b - 2)
                for half in range(2):
                    bk = nbank(bpool)
                    for kk in range(4):
                        k = half * 4 + kk
                        tr(psum[:, bk, kk * 128:(kk + 1) * 128], stg_in[b][:, k * 128:(k + 1) * 128], ident[:])
                    o = hh[:, half * 4:half * 4 + 4, b * 128:(b + 1) * 128]
                    i_ = bank(bk).rearrange("p (k t) -> p k t", t=128)
                    if b >= 2:
                        cp(o, i_, eng="dve")
                    else:
                        cp(o, i_, eng="act")
                    act(sqblk[b % 2][:, half * 4:half * 4 + 4, :], o, AF.Square)
            blk_stats(2)
            blk_stats(3)
            if dbg and "h_in" in dbg_out and ti == 0:
                dma(dbg_out["h_in"], hh[:], q="sp")
            pg.label = "A2_norm"
            norm_finish(hh, C_GMIX0, hnb, bst=B_O)

        def A_main(ti, hh, hnb):
            s_, st_ = divmod(ti, nst_seq)
            first = st_ == 0
            dslots = {0: gen_diag(0)}
            pg.label = "A4_inproj"
            if first:
                memset(glu[:, :, 0:30], 0.0)
            else:
                cp(kT[:, 0:128], kT[:, ST:ST + 128], eng="pool")
                cp(vtok[:, 0, :], vtok[:, NB, :], eng="pool")
                cp(glu[:, :, 0:30], glu[:, :, ST:ST + 30], eng="pool")
            w = use_slab(("in0", 0)).rearrange("p (k c) -> p k c", c=512)
            for c in range(4):
                bk = nbank(POOL_A)
                for k in range(KD):
                    mm(bank(bk), w[:, k, c * 128:(c + 1) * 128], hnb[:, k, :], k == 0, k == KD - 1)
                act(qT[:, c, :], bank(bk), AF.Identity, bias=col(C_BIN + c))
            w1 = use_slab(("in0", 1)).rearrange("p (k c) -> p k c", c=512)
            bk = nbank(POOL_A)
            for k in range(KD):
                mm(bank(bk), w1[:, k, 0:128], hnb[:, k, :], k == 0, k == KD - 1)
            act(kT[:, 128:128 + ST], bank(bk), AF.Identity, bias=col(C_BIN + 4))
            bk = nbank(POOL_A)
            for b in range(NB):
                o = psum[:, bk, b * 128:(b + 1) * 128]
                for k in range(KD):
                    mm(o, hnb[:, k, b * 128:(b + 1) * 128], w1[:, k, 128:256], k == 0, False)
                mm(o, ones_b[0:1, :], rows_b[0:1, R_BV:R_BV + 128], False, True)
            cp(vtok[:, 1:1 + NB, :], bank(bk).rearrange("p (b c) -> p b c", c=128), eng="act")

            def glu_pair(cc, wa, wg_):
                ba = nbank(POOL_A)
                for k in range(KD):
                    mm(bank(ba), wa[k], hnb[:, k, :], k == 0, k == KD - 1)
                bg = nbank(POOL_A)
                for k in range(KD):
                    mm(bank(bg), wg_[k], hnb[:, k, :], k == 0, k == KD - 1)
                sg_ = sig[cc % 2]
                act(sg_, bank(bg), AF.Sigmoid, bias=col(C_BIN + 7 + 2 * cc))
                stt(glu[:, cc, 30:30 + ST], bank(ba), col(C_BIN + 6 + 2 * cc), sg_, ALU.add, ALU.mult)

            glu_pair(0, [w1[:, k, 256:384] for k in range(KD)], [w1[:, k, 384:512] for k in range(KD)])

            wst_ = {}

            def f_glu1():
                w2 = use_slab(("in0", 2)).rearrange("p (k c) -> p k c", c=512)
                wst_["w2"] = w2
                glu_pair(1, [w2[:, k, 0:128] for k in range(KD)], [w2[:, k, 128:256] for k in range(KD)])

            def f_glu2():
                w2 = wst_["w2"]
                glu_pair(2, [w2[:, k, 256:384] for k in range(KD)], [w2[:, k, 384:512] for k in range(KD)])

            def f_glu3():
                w3 = use_slab(("in0", 3)).rearrange("p (k c) -> p k c", c=256)
                glu_pair(3, [w3[:, k, 0:128] for k in range(KD)], [w3[:, k, 128:256] for k in range(KD)])

            cbank = {}

            def f_conv(cc, half):
                def f():
                    lab = pg.label
                    pg.label = "A6_conv"
                    if half == 0:
                        cbank[cc] = nbank(POOL_A)
                        if cc + 1 < 4:
                            dslots[cc + 1] = gen_diag(cc + 1)
                    bk_ = cbank[cc]
                    for j in (range(0, 16) if half == 0 else range(16, 31)):
                        mm(bank(bk_), diagbuf[:, dslots[cc][j], :], glu[:, cc, j:j + ST], j == 0, j == 30)
                    if half == 1:
                        act(cv[:, cc, :], bank(bk_), AF.Identity, bias=col(C_CONVB + cc))
                        act(cvb[:, cc, :], bank(bk_), AF.Identity, bias=col(C_CONVB + cc))
                        act(cvq[:, cc, :], cv[:, cc, :], AF.Square)
                    pg.label = lab
                return f

            def f_ln():
                lab = pg.label
                pg.label = "A6_conv"
                bm = nbank(POOL_A)
                for cc in range(4):
                    mm(bank(bm), onesm_b[:], cvb[:, cc, :], cc == 0, cc == 3)
                for cc in range(4):
                    mm(bank(B_ST), onesm_b[:], cvq[:, cc, :], cc == 0, cc == 3)
                m2 = lnt[0]
                act(m2, bank(bm), AF.Square)
                tt(m2, bank(B_ST), m2, ALU.subtract)
                act(m2, m2, AF.Sqrt, bias=col_eps[:], scale=1.0)
                recip(bank(B_ST), m2)
                for cc in range(4):
                    d_ = lnt[1 + cc % 2]
                    tt(d_, cv[:, cc, :], bank(bm), ALU.subtract)
                    stt(d_, d_, col(C_CLNG + cc), bank(B_ST), ALU.mult, ALU.mult)
                    act(cat[:, 4 + cc, :], d_, AF.Silu, bias=col(C_CLNB + cc))
                pg.label = lab

            fillers = [f_glu1, f_glu2, f_glu3]
            for cc in range(4):
                fillers += [f_conv(cc, 0), f_conv(cc, 1)]
            fillers.append(f_ln)

            def filler():
                if fillers:
                    fillers.pop(0)()

            pg.label = "A5_attn"
            S4 = psum[:, B_S0:B_S0 + 2, :].rearrange("p a (g s) -> p (a g) s", s=256)

            def att_S(p):
                b, kv = divmod(p, 2)
                ph = kv * 64
                mi = 1 if (first and b == 0) else 0
                mk = mask2[:, mi:mi + 1, :].broadcast_to([128, 4, 256])
                for g in range(4):
                    mm(S4[:, g, :], qT[ph:ph + 64, g, b * 128:(b + 1) * 128],
                       kT[ph:ph + 64, b * 128:b * 128 + 256], True, True)
                sm_ = Sm[p % 2]
                tt(sm_[:, :, :], S4[:, :, :], mk, ALU.add)
                so = (p % 2) * 32
                mx = small[:, so + 0:so + 4]
                negm = small[:, so + 4:so + 8]
                dd = small[:, so + 8:so + 12]
                es_ = small[:, so + 12:so + 16]
                bo = (b % 2) * 16 + kv * 4
                ll = st8[:, bo:bo + 4]
                rinv = st8[:, 32 + bo:32 + bo + 4]
                reduce_max(mx, sm_[:, :, :])
                stt(negm, mx, -0.125, nsink[:, kv * 4:kv * 4 + 4], ALU.mult, ALU.min)
                tt(dd, cols[:, C_SINK + kv * 4:C_SINK + kv * 4 + 4], negm, ALU.add)
                for g in range(4):
                    act(Pb[p % 2][:, g, :], sm_[:, g, :], AF.Exp, bias=negm[:, g:g + 1], scale=0.125,
                        accum=ll[:, g:g + 1])
                act(es_, dd, AF.Exp)
                tt(ll, ll, es_, ALU.add)
                recip(rinv, ll)

            def att_T(p):
                PTp = bank(B_PT).bitcast(BF).rearrange("p (a g q) -> p a g q", g=4, q=128)
                for a in range(2):
                    for g in range(4):
                        tr(PTp[:, a, g, :], Pb[p % 2][:, g, a * 128:(a + 1) * 128], identb[:])
                cp(PT[p % 2][:], bank(B_PT).bitcast(BF).rearrange("p (a c) -> p a c", c=512), eng="act")

            def att_PV(p):
                b, kv = divmod(p, 2)
                ph = kv * 64
                for g in range(4):
                    o = psum[:, B_O, g * 128 + ph:g * 128 + ph + 64]
                    for a in range(2):
                        mm(o, PT[p % 2][:, a, g * 128:(g + 1) * 128], vtok[:, b + a, ph:ph + 64], a == 0, a == 1)
                if kv == 1:
                    bo = (b % 2) * 16
                    rbc = st8[:, 32 + bo:32 + bo + 8].rearrange("p (kv g) -> p g kv", g=4) \
                        .unsqueeze(3).broadcast_to([128, 4, 2, 64])
                    tt(Osb.rearrange("p (g kv d) -> p g kv d", kv=2, d=64),
                       bank(B_O).rearrange("p (g kv d) -> p g kv d", kv=2, d=64), rbc, ALU.mult)
                    bk_ = nbank(POOL_A)
                    ov = bank(bk_).bitcast(BF)[:, 0:512].rearrange("p (g q) -> p g q", q=128)
                    for g in range(4):
                        tr(ov[:, g, :], Osb[:, g * 128:(g + 1) * 128], identb[:])
                    cp(cat[:, 0:4, b * 128:(b + 1) * 128], ov, eng="act")

            NP = 2 * NB
            att_S(0)
            filler()
            for p in range(NP):
                if p + 1 < NP:
                    att_S(p + 1)
                filler()
                att_T(p)
                if p % 2 == 1:
                    filler()
                if p > 0:
                    att_PV(p - 1)
            while fillers:
                filler()
            att_PV(NP - 1)
            pg.label = "A7_out"
            proj_resid(hh, cat, [("out0", 0), ("out0", 1)], sqrot_a)

        def phase_C(ti, hh, hnb):
            s_, st_ = divmod(ti, nst_seq)
            first = st_ == 0
            pg.label = "C1_norm"
            norm_finish(hh, C_GMIX1, hnb)
            pg.label = "C2_in"
            if not first:
                cp(zptok[:, 0, :], zptok[:, NB, :], eng="pool")
            w = use_slab(("in1", 0)).rearrange("p (k c) -> p k c", c=512)
            for b in range(NB):
                bk = nbank(POOL_A)
                for k in range(KD):
                    mm(bank(bk), hnb[:, k, b * 128:(b + 1) * 128], w[:, k, :], k == 0, k == KD - 1)
                cp(zptok[:, 1 + b, :], bank(bk), eng=("act" if b % 2 else "dve"))
            wv = use_slab(("in1", 1)).rearrange("p (k c) -> p k c", c=512)
            for b in range(NB):
                bk = nbank(POOL_A)
                for k in range(KD):
                    mm(bank(bk), hnb[:, k, b * 128:(b + 1) * 128], wv[:, k, :], k == 0, k == KD - 1)
                vg_ = vg[b % 2]
                vn_ = vn[b % 2]
                act(vg_, bank(bk), AF.Gelu_apprx_tanh)
                st6 = smallc[:, b * 16:b * 16 + 6]
                mv = smallc[:, b * 16 + 8:b * 16 + 10]
                sd = smallc[:, b * 16 + 10:b * 16 + 11]
                pg.add("dve", lambda e, o=st6, i_=vg_: e.bn_stats(out=o, in_=i_), reads=[vg_], writes=[st6])
                pg.add("dve", lambda e, o=mv, i_=st6: e.bn_aggr(out=o, in_=i_), reads=[st6], writes=[mv])
                act(sd, mv[:, 1:2], AF.Sqrt, bias=col_eps[:], scale=1.0)
                recip(sd, sd)
                ts(vn_, vg_, mv[:, 0:1], ALU.subtract, sd, ALU.mult)
                tt(vn_, vn_, bcs[:, 0, :], ALU.mult, eng="pool")
                tt(vln[:, b, :], vn_, bcs[:, 1, :], ALU.add, eng="pool")
            w = use_slab(("in1", 2)).rearrange("p (k c) -> p k c", c=512)
            for c in range(4):
                bk = nbank(POOL_A)
                for k in range(KD):
                    mm(bank(bk), w[:, k, c * 128:(c + 1) * 128], hnb[:, k, :], k == 0, k == KD - 1)
                act(uT[:, c, :], bank(bk), AF.Gelu_apprx_tanh)
            pg.label = "C3_pool"
            for g in range(4):
                bk = nbank(POOL_A)
                for b in range(NB):
                    o = psum[:, bk, b * 128:(b + 1) * 128]
                    if first and b == 0:
                        mm(o, zptok[:, 1 + b, g * 128:(g + 1) * 128], poolm[:, 8 + g, :], True, True)
                    else:
                        mm(o, zptok[:, 1 + b, g * 128:(g + 1) * 128], poolm[:, g, :], True, False)
                        mm(o, zptok[:, b, g * 128:(g + 1) * 128], poolm[:, 4 + g, :], False, True)
                cp(pooled[:, g, :], bank(bk), eng=("act" if g % 2 else "dve"))
            for g in range(4):
                bk2 = nbank(POOL_A)
                mm(bank(bk2), wpool_b[:, g, :], pooled[:, g, :], True, True)
                act(cat[:, g, :], bank(bk2), AF.Identity, scale=col(C_PSCALE + g))
            pg.label = "C4_sgu"
            for g in range(4):
                bk = nbank(POOL_A)
                for b in range(NB):
                    o = psum[:, bk, b * 128:(b + 1) * 128]
                    mm(o, vln[:, b, g * 128:(g + 1) * 128], wst_b[:, g, :], True, False)
                    mm(o, ones_b[0:1, :], rows_b[0:1, R_BS + g * 128:R_BS + (g + 1) * 128], False, True)
                tt(cat[:, 4 + g, :], bank(bk), uT[:, g, :], ALU.mult)
            pg.label = "C5_out"
            proj_resid(hh, cat, [("out1", 0), ("out1", 1)], sqrot_c)

        def E_pre(ti, hh):
            pg.label = "E_final"
            act(rstd_s, bank(B_ST), AF.Sqrt, bias=col_eps[:], scale=1.0 / D)
            recip(bank(B_ST), rstd_s)
            for k in range(KD):
                stt(hh[:, k, :], hh[:, k, :], col(C_GFIN + k), bank(B_ST), ALU.mult, ALU.mult)

        def E_post(ti, hh):
            s_, st_ = divmod(ti, nst_seq)
            pg.label = "E_final"
            for b in range(NB):
                so = stg_out[b % 2]
                for half in range(2):
                    bk = nbank(POOL_F)
                    for kk in range(4):
                        k = half * 4 + kk
                        tr(psum[:, bk, kk * 128:(kk + 1) * 128], hh[:, k, b * 128:(b + 1) * 128], ident[:])
                    if half == 0:
                        cp(so[:, 0:512], bank(bk), eng="act")
                    else:
                        cp(so[:, 512:1024], bank(bk), eng="dve")
                t0 = st_ * ST + b * 128
                dma(out[s_, t0:t0 + 128, :], so[:], q="pool", is_out=True)

        load_x(0)
        A_pre(0, h[0], hn[0], POOL_A)
        for ti in range(ntiles):
            hh = h[ti % 2]
            if ti + 1 < ntiles:
                load_x(ti + 1)
            A_main(ti, hh, hn[0])
            if dbg and "h_a" in dbg_out and ti == 0:
                dma(dbg_out["h_a"], hh[:], q="sp")
            ffn_gu(hh, 0, hn[1], pre_hook=((lambda t=ti: E_post(t - 1, h[(t - 1) % 2])) if ti > 0 else None))
            ffn_dn(hh, 0)
            if dbg and "h_b" in dbg_out and ti == 0:
                dma(dbg_out["h_b"], hh[:], q="sp")
            phase_C(ti, hh, hn[0])
            if dbg and "h_c" in dbg_out and ti == 0:
                dma(dbg_out["h_c"], hh[:], q="sp")
            ffn_gu(hh, 1, hn[1],
                   pre_hook=((lambda t=ti: A_pre(t + 1, h[(t + 1) % 2], hn[0], [0, 1, 2, 3, 4, 5])) if ti + 1 < ntiles else None))
            ffn_dn(hh, 1)
            E_pre(ti, hh)
        E_post(ntiles - 1, h[(ntiles - 1) % 2])
        pg.emit_all(block, sems, dsems)
    return nc, pg


_CACHE = {}


def kernel(**inputs):
    xs = np.asarray(inputs["x"])
    B, S, _ = xs.shape
    nseq = B // NCORES
    maps = prep_inputs(inputs, NCORES, nseq)
    key = (nseq, S)
    if key not in _CACHE:
        _CACHE[key] = build(nseq, S)[0]
    nc = _CACHE[key]
    res = run_bass_kernel_spmd(nc, maps, core_ids=list(range(NCORES)))
    outs = [np.asarray(r["out"]) for r in res.results]
    return np.concatenate(outs, axis=0).astype(np.float32)
```

```python
import numpy as np
import concourse.bass as bass
import concourse.mybir as mybir
from concourse.bass_utils import run_bass_kernel_spmd

F32 = mybir.dt.float32
BF = mybir.dt.bfloat16
AF = mybir.ActivationFunctionType
ALU = mybir.AluOpType
AX = mybir.AxisListType

NCORES = 8
D = 1024
KD = 8
DFF = 2816
NF = 22
ST = 512
NB = 4
EPS = 1e-5
SLAB = 4096
NSLAB = 49
RING = 4
NDSEM = 8

C_GMIX0, C_GMIX1, C_GFFN0, C_GFFN1, C_GFIN = 0, 8, 16, 24, 32
C_BIN = 40
C_CONVW = 54
C_CONVB = 178
C_CLNG = 182
C_CLNB = 186
C_PSCALE = 190
C_SINK = 194
NCOL = 202
K_IDENT = 0
K_MASK = 128
K_TRIL = 640
K_POOL = 768
NCST = 768 + 1536
R_BV = 0
R_BS = 128
NROW = 640


def _esize(dt):
    return 2 if dt == BF else 4


class Op:
    __slots__ = ("id", "stream", "emit", "deps", "sig", "cnt", "dma", "dsem", "dval", "label")


class Prog:
    STREAMS = ("pe", "act", "dve", "pool", "sp")

    def __init__(self):
        self.nc = bass.Bass("TRN2", target_bir_lowering=False)
        self.ops = []
        self.streams = {s: [] for s in self.STREAMS}
        self.track = {}
        self.ndma = {s: 0 for s in self.STREAMS}
        self.dma_ops = {s: [] for s in self.STREAMS}
        self.out_dmas = []
        self.label = ""

    @staticmethod
    def _rng(ap):
        t = ap.tensor
        es = _esize(ap.dtype)
        dims = ap.ap
        space = str(ap.space)
        if "DRAM" in space.upper() or "HBM" in space.upper():
            lo = ap.offset
            ext = 0
            for (stp, cnt) in dims:
                ext += (cnt - 1) * abs(stp)
            return (t.name, lo * es, (lo + ext + 1) * es, 0, 128)
        pstep, pcnt = dims[0]
        p0 = ap.start_partition()
        lo = ap.offset - p0 * pstep
        ext = 0
        for (stp, cnt) in dims[1:]:
            ext += (cnt - 1) * abs(stp)
        if "PSUM" in space.upper():
            b0 = (lo * es) // 2048
            b1 = ((lo + ext + 1) * es + 2047) // 2048
            return ("~psum", b0 * 2048, b1 * 2048, 0, 128)
        return (t.name, lo * es, (lo + ext + 1) * es, p0, p0 + pcnt)

    def add(self, stream, emit, reads=(), writes=(), dma=False, is_out=False):
        op = Op()
        op.id = len(self.ops)
        op.stream = stream
        op.emit = emit
        op.deps = set()
        op.sig = False
        op.cnt = 0
        op.dma = dma
        op.dsem = None
        op.dval = 0
        op.label = self.label
        if dma:
            n = self.ndma[stream]
            self.ndma[stream] = n + 1
            op.dsem = n % NDSEM
            op.dval = 16 * (n // NDSEM + 1)
            if n >= NDSEM:
                op.deps.add(self.dma_ops[stream][n - NDSEM].id)
            self.dma_ops[stream].append(op)
            if is_out:
                self.out_dmas.append(op)
        rr = [self._rng(a) for a in reads if a is not None]
        ww = [self._rng(a) for a in writes if a is not None]
        for (name, lo, hi, p0, p1) in rr:
            excl = name == "~psum"
            for rec in self.track.get(name, ()):
                if (rec[5] or (excl and rec[6][0] != stream)) and rec[0] < hi and lo < rec[1] \
                        and rec[2] < p1 and p0 < rec[3]:
                    op.deps.add(rec[4])
        for (name, lo, hi, p0, p1) in ww:
            for rec in self.track.get(name, ()):
                if rec[0] < hi and lo < rec[1] and rec[2] < p1 and p0 < rec[3]:
                    op.deps.add(rec[4])
        op.deps.discard(op.id)
        ekey = (stream, dma and op.id)
        for (name, lo, hi, p0, p1) in ww:
            lst = self.track.setdefault(name, [])
            lst[:] = [r for r in lst if not (lo <= r[0] and r[1] <= hi and p0 <= r[2] and r[3] <= p1)]
            lst.append((lo, hi, p0, p1, op.id, True, ekey))
        for (name, lo, hi, p0, p1) in rr:
            lst = self.track.setdefault(name, [])
            if not dma:
                lst[:] = [r for r in lst if not ((not r[5]) and r[6] == ekey and lo <= r[0] and r[1] <= hi
                                                 and p0 <= r[2] and r[3] <= p1)]
            lst.append((lo, hi, p0, p1, op.id, False, ekey))
        self.ops.append(op)
        self.streams[stream].append(op)
        return op

    def emit_all(self, block, sems, dsems):
        ops = self.ops
        for op in ops:
            for d in op.deps:
                dop = ops[d]
                if not dop.dma:
                    if dop.stream == "pe" and op.stream == "pe" and not op.dma:
                        continue
                    dop.sig = True
        for s in self.STREAMS:
            c = 0
            for op in self.streams[s]:
                if (not op.dma) and op.sig:
                    c += 1
                    op.cnt = c
        nwaits = {s: 0 for s in self.STREAMS}

        def run_stream(s, eng):
            seen = {}
            for op in self.streams[s]:
                waits = {}
                for d in op.deps:
                    dop = ops[d]
                    if dop.dma:
                        key = ("d", dop.stream, dop.dsem)
                        val = dop.dval
                    else:
                        if dop.stream == "pe" and s == "pe" and not op.dma:
                            continue
                        key = ("c", dop.stream)
                        val = dop.cnt
                    if waits.get(key, 0) < val:
                        waits[key] = val
                for key, val in waits.items():
                    if seen.get(key, 0) >= val:
                        continue
                    seen[key] = val
                    sem = dsems[key[1]][key[2]] if key[0] == "d" else sems[key[1]]
                    eng.wait_ge(sem, val)
                    nwaits[s] += 1
                ins = op.emit(eng)
                if op.dma:
                    ins.then_inc(dsems[s][op.dsem], 16)
                elif op.sig:
                    ins.then_inc(sems[s], 1)
            if s == "sp":
                fin = {}
                for op in self.out_dmas:
                    key = (op.stream, op.dsem)
                    fin[key] = max(fin.get(key, 0), op.dval)
                for (st_, di), val in fin.items():
                    eng.wait_ge(dsems[st_][di], val)

        @block.tensor
        def _(e):
            run_stream("pe", e)

        @block.scalar
        def _(e):
            run_stream("act", e)

        @block.vector
        def _(e):
            run_stream("dve", e)

        @block.gpsimd
        def _(e):
            run_stream("pool", e)

        @block.sync
        def _(e):
            run_stream("sp", e)

        self.nwaits = nwaits


def perm_q():
    idx = np.zeros(512, dtype=np.int64)
    for c in range(4):
        for half in range(2):
            for d in range(64):
                idx[c * 128 + half * 64 + d] = (half * 4 + c) * 64 + d
    return idx


def perm_in0():
    pq = perm_q()
    cols = list(pq) + list(range(512, 768))
    for cc in range(4):
        cols += list(range(768 + cc * 128, 768 + (cc + 1) * 128))
        cols += list(range(1280 + cc * 128, 1280 + (cc + 1) * 128))
    return np.array(cols, dtype=np.int64)


def host_consts():
    cst = np.zeros((128, NCST), dtype=np.float32)
    cst[:, K_IDENT:K_IDENT + 128] = np.eye(128, dtype=np.float32)
    qi = np.arange(128)[:, None]
    r = np.arange(256)[None, :]
    dist = qi + 128 - r
    band = (dist >= 0) & (dist < 128)
    NEG = -30000.0
    cst[:, K_MASK:K_MASK + 256] = np.where(band, 0.0, NEG)
    first = band & (r >= 128)
    cst[:, K_MASK + 256:K_MASK + 512] = np.where(first, 0.0, NEG)
    s = np.arange(128)[:, None]
    t = np.arange(128)[None, :]
    cst[:, K_TRIL:K_TRIL + 128] = (s <= t).astype(np.float32)
    for g, w in enumerate((2, 4, 8, 16)):
        cur = np.where((t - s >= 0) & (t - s < w), 1.0 / w, 0.0) - (s == t)
        prev = np.where((t + 128 - s) < w, 1.0 / w, 0.0)
        cnt = np.minimum(t + 1, w).astype(np.float64)
        fst = np.where((t - s >= 0) & (t - s < w), 1.0 / cnt, 0.0) - (s == t)
        cst[:, K_POOL + g * 128:K_POOL + (g + 1) * 128] = cur
        cst[:, K_POOL + (4 + g) * 128:K_POOL + (5 + g) * 128] = prev
        cst[:, K_POOL + (8 + g) * 128:K_POOL + (9 + g) * 128] = fst
    return cst


def prep_inputs(inp, ncores, nseq):
    f = lambda a: np.ascontiguousarray(np.asarray(a, dtype=np.float32))
    x = f(inp["x"])
    pin = perm_in0()
    pq = perm_q()
    a_w_in = f(inp["a_w_in"])[0][:, pin]
    a_b_in = f(inp["a_b_in"])[0][pin]
    a_w_out = f(inp["a_w_out"])[0]
    a_w_out = np.concatenate([a_w_out[pq], a_w_out[512:]], axis=0)
    cols = np.zeros((128, NCOL), dtype=np.float32)
    colv = lambda v: np.asarray(v, dtype=np.float32).reshape(-1, 128).T
    cols[:, C_GMIX0:C_GMIX0 + 8] = colv(inp["mix_norm"][0])
    cols[:, C_GMIX1:C_GMIX1 + 8] = colv(inp["mix_norm"][1])
    cols[:, C_GFFN0:C_GFFN0 + 8] = colv(inp["ffn_norm"][0])
    cols[:, C_GFFN1:C_GFFN1 + 8] = colv(inp["ffn_norm"][1])
    cols[:, C_GFIN:C_GFIN + 8] = colv(inp["final_norm"])
    cols[:, C_BIN:C_BIN + 14] = colv(a_b_in)
    cw = f(inp["a_conv_w"])[0]
    cols[:, C_CONVW:C_CONVW + 124] = cw.T.reshape(4, 128, 31).transpose(1, 0, 2).reshape(128, 124)
    cols[:, C_CONVB:C_CONVB + 4] = colv(inp["a_conv_b"][0])
    cols[:, C_CLNG:C_CLNG + 4] = colv(inp["a_cln_g"][0])
    cols[:, C_CLNB:C_CLNB + 4] = colv(inp["a_cln_b"][0])
    cols[:, C_PSCALE:C_PSCALE + 4] = colv(inp["c_pool_scale"][0])
    cols[:, C_SINK:C_SINK + 8] = np.broadcast_to(f(inp["a_sinks"])[0][None, :], (128, 8))
    rows = np.zeros((1, NROW), dtype=np.float32)
    rows[0, R_BV:R_BV + 128] = a_b_in[640:768]
    rows[0, R_BS:R_BS + 512] = f(inp["c_b_s"])[0].reshape(512)
    bc = np.zeros((128, 2, 512), dtype=np.float32)
    bc[:, 0, :] = np.broadcast_to(f(inp["c_sln_g"])[0][None, :], (128, 512))
    bc[:, 1, :] = np.broadcast_to(f(inp["c_sln_b"])[0][None, :], (128, 512))
    wpool = np.ascontiguousarray(f(inp["c_w_pool"])[0].transpose(1, 0, 2))
    wst = np.ascontiguousarray(f(inp["c_w_s"])[0].transpose(2, 0, 1))
    shared = {
        "w_in0": np.ascontiguousarray(a_w_in), "w_out0": np.ascontiguousarray(a_w_out),
        "w_in1": f(inp["c_w_in"])[0], "w_out1": f(inp["c_w_out"])[0],
        "wg": f(inp["ffn_w_gate"]), "wu": f(inp["ffn_w_up"]), "wd": f(inp["ffn_w_down"]),
        "cols": cols, "rows": rows, "bc": bc, "wpool": wpool, "wst": wst, "cst": host_consts(),
    }
    maps = []
    for c in range(ncores):
        m = dict(shared)
        m["x"] = np.ascontiguousarray(x[c * nseq:(c + 1) * nseq])
        maps.append(m)
    return maps


def build(nseq, seqlen, dbg=None):
    pg = Prog()
    nc = pg.nc
    nst_seq = seqlen // ST
    ntiles = nseq * nst_seq
    dram_in = lambda name, shape: nc.dram_tensor(name, shape, F32, kind="ExternalInput").ap()
    x = dram_in("x", [nseq, seqlen, D])
    w_in0 = dram_in("w_in0", [D, 1792])
    w_out0 = dram_in("w_out0", [D, D])
    w_in1 = dram_in("w_in1", [D, 1536])
    w_out1 = dram_in("w_out1", [D, D])
    wg = dram_in("wg", [2, D, DFF])
    wu = dram_in("wu", [2, D, DFF])
    wd = dram_in("wd", [2, DFF, D])
    cols_d = dram_in("cols", [128, NCOL])
    rows_d = dram_in("rows", [1, NROW])
    bc_d = dram_in("bc", [128, 2, 512])
    wpool_d = dram_in("wpool", [128, 4, 128])
    wst_d = dram_in("wst", [128, 4, 128])
    cst_d = dram_in("cst", [128, NCST])
    out = nc.dram_tensor("out", [nseq, seqlen, D], F32, kind="ExternalOutput").ap()
    wscr = nc.dram_tensor("wscr", [NSLAB, 128, SLAB], BF).ap()
    dbg_out = {}
    if dbg:
        for name, shape in dbg.items():
            dbg_out[name] = nc.dram_tensor("dbg_" + name, shape, F32, kind="ExternalOutput").ap()

    import contextlib
    es = contextlib.ExitStack()
    with es:
        sb = lambda name, shape, dt: es.enter_context(nc.sbuf_tensor(name, shape, dt))
        cols = sb("cols_s", [128, NCOL], F32)
        nsink = sb("nsink", [128, 8], F32)
        ident = sb("ident", [128, 128], F32)
        identb = sb("identb", [128, 128], BF)
        mask2 = sb("mask2", [128, 2, 256], F32)
        ones_b = sb("ones_b", [128, 128], BF)
        onesm_b = sb("onesm_b", [128, 128], BF)
        rows_b = sb("rows_b", [1, NROW], BF)
        bcs = sb("bcs", [128, 2, 512], F32)
        poolm = sb("poolm", [128, 12, 128], BF)
        wpool_b = sb("wpool_b", [128, 4, 128], BF)
        wst_b = sb("wst_b", [128, 4, 128], BF)
        NDIAG = 64
        diagbuf = sb("diagbuf", [128, NDIAG, 128], BF)
        col_eps = sb("col_eps", [128, 1], F32)
        ring = sb("ring", [128, RING, SLAB], BF)
        h = [sb("h%d" % i, [128, KD, ST], F32) for i in range(2)]
        hn = [sb("hn%d" % i, [128, KD, ST], BF) for i in range(2)]
        stg_in = [sb("stgi%d" % i, [128, D], F32) for i in range(4)]
        stg_out = [sb("stgo%d" % i, [128, D], F32) for i in range(2)]
        kT = sb("kT", [128, 128 + ST], BF)
        vtok = sb("vtok", [128, 1 + NB, 128], BF)
        glu = sb("glu", [128, 4, 30 + ST], BF)
        zptok = sb("zptok", [128, 1 + NB, 512], BF)
        qT = sb("qT", [128, 4, ST], BF)
        cat = sb("cat", [128, KD, ST], BF)
        arena = sb("arena", [128, 24 * 1024], BF)
        psum = es.enter_context(nc.psum_tensor("psum", [128, 8, 512], F32))
        sems = {s: es.enter_context(nc.semaphore("sem_" + s)) for s in Prog.STREAMS}
        dsems = {s: [es.enter_context(nc.semaphore("dsem_%s%d" % (s, i))) for i in range(NDSEM)]
                 for s in ("sp", "pool", "act")}
        block = es.enter_context(nc.Block())

        def aview(off_b, nelem, dt):
            o = off_b // 2
            n = nelem * _esize(dt) // 2
            v = arena[:, o:o + n]
            return v if dt == BF else v.bitcast(dt)

        KB = 1024
        sqrot_a = [aview(i * KB, 512, BF) for i in range(2)]
        sqblk = [aview(32 * KB + i * 2 * KB, 1024, BF).rearrange("p (k t) -> p k t", t=128) for i in range(2)]
        small = aview(6 * KB, 256, F32)
        rstd_s = aview(30 * KB, 512, F32)
        sig = [aview(10 * KB + i * 2 * KB, 512, F32) for i in range(2)]
        Sm = [aview(14 * KB + i * 4 * KB, 1024, F32).rearrange("p (g s) -> p g s", s=256) for i in range(2)]
        Osb = aview(44 * KB, 512, BF)
        st8 = aview(45 * KB, 256, F32)
        Pb = [aview(22 * KB + i * 2 * KB, 1024, BF).rearrange("p (g s) -> p g s", s=256) for i in range(2)]
        PT = [aview(o_ * KB, 1024, BF).rearrange("p (a c) -> p a c", c=512) for o_ in (26, 46)]
        cv = aview(28 * KB, 2048, F32).rearrange("p (c t) -> p c t", t=ST)
        cvb = aview(36 * KB, 2048, BF).rearrange("p (c t) -> p c t", t=ST)
        cvq = aview(40 * KB, 2048, BF).rearrange("p (c t) -> p c t", t=ST)
        lnt = [aview(i * 2 * KB, 512, F32) for i in range(3)]
        actb = aview(0, NF * ST, BF).rearrange("p (f t) -> p f t", t=ST)
        sgt = [aview(22 * KB + i * KB, 512, BF) for i in range(4)]
        sqrot_f = [aview(26 * KB + i * KB, 512, BF) for i in range(2)]
        uT = aview(0, 2048, BF).rearrange("p (c t) -> p c t", t=ST)
        vg = [aview(4 * KB + i * 2 * KB, 512, F32) for i in range(2)]
        vn = [aview(8 * KB + i * 2 * KB, 512, F32) for i in range(2)]
        vln = aview(12 * KB, 2048, BF).rearrange("p (b c) -> p b c", c=512)
        pooled = aview(16 * KB, 2048, BF).rearrange("p (c t) -> p c t", t=ST)
        sqrot_c = [aview(20 * KB + i * KB, 512, BF) for i in range(2)]
        smallc = aview(28 * KB, 256, F32)
        pstg = [aview(i * 16 * KB, SLAB, F32) for i in range(2)]
        pbf = [aview(32 * KB + i * 8 * KB, SLAB, BF) for i in range(2)]
        cstg = aview(8 * KB, 1536, F32)

        def mm(o, lhsT, rhs, start, stop):
            op = pg.add("pe", lambda e: e.matmul(o, lhsT=lhsT, rhs=rhs, start=start, stop=stop),
                        reads=[lhsT, rhs], writes=[o])
            op.label = (op.label, "MATMUL %d*%d*%d" % (lhsT.shape[0], lhsT.shape[1], rhs.shape[-1]))

        def tr(o, in_, idn):
            op = pg.add("pe", lambda e: e.transpose(o, in_, idn), reads=[in_, idn], writes=[o])
            op.label = (op.label, "TR")

        def act(o, in_, func, bias=None, scale=None, accum=None):
            kw = {}
            rd = [in_]
            if bias is not None:
                kw["bias"] = bias
                if not isinstance(bias, float):
                    rd.append(bias)
            if scale is not None:
                kw["scale"] = scale
                if not isinstance(scale, float):
                    rd.append(scale)
            wr = [o]
            if accum is not None:
                kw["accum_out"] = accum
                wr.append(accum)
            pg.add("act", lambda e: e.activation(out=o, in_=in_, func=func, **kw), reads=rd, writes=wr)

        def tt(o, a, b, op, eng="dve"):
            pg.add(eng, lambda e: e.tensor_tensor(out=o, in0=a, in1=b, op=op), reads=[a, b], writes=[o])

        def ts(o, a, s1, op0, s2=None, op1=None, eng="dve"):
            rd = [a]
            if not isinstance(s1, float):
                rd.append(s1)
            if s2 is not None and not isinstance(s2, float):
                rd.append(s2)
            if op1 is None:
                pg.add(eng, lambda e: e.tensor_scalar(out=o, in0=a, scalar1=s1, scalar2=None, op0=op0),
                       reads=rd, writes=[o])
            else:
                pg.add(eng, lambda e: e.tensor_scalar(out=o, in0=a, scalar1=s1, scalar2=s2, op0=op0, op1=op1),
                       reads=rd, writes=[o])

        def stt(o, a, s, b, op0, op1):
            rd = [a, b]
            if not isinstance(s, float):
                rd.append(s)
            pg.add("dve", lambda e: e.scalar_tensor_tensor(out=o, in0=a, scalar=s, in1=b, op0=op0, op1=op1),
                   reads=rd, writes=[o])

        def cp(o, a, eng="dve"):
            if eng == "act":
                pg.add("act", lambda e: e.activation(out=o, in_=a, func=AF.Copy), reads=[a], writes=[o])
            else:
                pg.add(eng, lambda e: e.tensor_copy(out=o, in_=a), reads=[a], writes=[o])

        def memset(o, val, eng="dve"):
            pg.add(eng, lambda e: e.memset(o, val), reads=[], writes=[o])

        def recip(o, a):
            pg.add("dve", lambda e: e.reciprocal(out=o, in_=a), reads=[a], writes=[o])

        def reduce_max(o, a):
            pg.add("dve", lambda e: e.tensor_reduce(out=o, in_=a, axis=AX.X, op=ALU.max), reads=[a], writes=[o])

        def dma(o, in_, q="sp", rd_track=True, is_out=False):
            reads = [in_] if rd_track else []
            if q == "sp":
                pg.add("sp", lambda e: e.dma_start(out=o, in_=in_), reads=reads, writes=[o], dma=True, is_out=is_out)
            elif q == "pool":
                pg.add("pool", lambda e: e.dma_start(out=o, in_=in_), reads=reads, writes=[o], dma=True, is_out=is_out)
            else:
                pg.add("act", lambda e: e.dma_start(out=o, in_=in_), reads=reads, writes=[o], dma=True, is_out=is_out)

        def col(c):
            return cols[:, c:c + 1]

        bank = lambda b: psum[:, b, :]
        _rr = {"i": 0}

        def nbank(pool):
            _rr["i"] += 1
            return pool[_rr["i"] % len(pool)]

        POOL_A = [0, 1, 2]
        POOL_F = [0, 1, 2, 3, 4, 5, 6]
        B_S0, B_PT, B_O, B_ST = 3, 5, 6, 7

        dma(cols[:], cols_d, rd_track=False)
        dma(bcs[:], bc_d, rd_track=False)
        dma(cstg[:, 0:640], cst_d[:, 0:640], rd_track=False)
        cp(ident[:], cstg[:, K_IDENT:K_IDENT + 128])
        cp(identb[:], cstg[:, K_IDENT:K_IDENT + 128])
        cp(mask2[:], cstg[:, K_MASK:K_MASK + 512].rearrange("p (a s) -> p a s", s=256))
        memset(ones_b[:], 1.0)
        memset(onesm_b[:], 1.0 / 512.0)
        ts(nsink[:], cols[:, C_SINK:C_SINK + 8], -1.0, ALU.mult)
        trilf = aview(14 * KB, 128, F32)
        wsf = aview(0, 512, F32).rearrange("p (g t) -> p g t", t=128)
        dma(trilf, cst_d[:, K_TRIL:K_TRIL + 128], rd_track=False)
        dma(wsf, wst_d, rd_track=False)
        for g in range(4):
            tt(wst_b[:, g, :], wsf[:, g, :], trilf, ALU.mult)
        wpf = aview(2 * KB, 512, F32).rearrange("p (g t) -> p g t", t=128)
        dma(wpf, wpool_d, rd_track=False)
        cp(wpool_b[:], wpf)
        rowf = aview(4 * KB, NROW, F32)[0:1, :]
        dma(rowf, rows_d, rd_track=False)
        cp(rows_b[:], rowf)
        dma(cstg[:], cst_d[:, K_POOL:K_POOL + 1536], rd_track=False)
        cp(poolm[:], cstg[:].rearrange("p (m t) -> p m t", t=128))
        memset(col_eps[:], EPS)
        memset(kT[:, 0:128], 0.0)
        memset(vtok[:, 0, :], 0.0)
        memset(zptok[:, 0, :], 0.0)
        dstate = {"i": 0}

        slab_no = {"i": 0}

        slab_src = {}

        def conv_slab(srcs, width):
            j = slab_no["i"]
            slab_no["i"] += 1
            slab_src[j] = (srcs, width)
            return j

        slabs = {}
        w_in0_v = w_in0.rearrange("(k p) f -> p k f", p=128)
        w_out0_v = w_out0.rearrange("(k p) f -> p k f", p=128)
        w_in1_v = w_in1.rearrange("(k p) f -> p k f", p=128)
        w_out1_v = w_out1.rearrange("(k p) f -> p k f", p=128)
        for j in range(4):
            nco = 512 if j < 3 else 256
            slabs[("in0", j)] = conv_slab([(8, 0, nco, w_in0_v[:, :, j * 512:j * 512 + nco])], nco)
        for j in range(2):
            slabs[("out0", j)] = conv_slab([(8, 0, 512, w_out0_v[:, :, j * 512:(j + 1) * 512])], 512)

        def ffn_slabs(l):
            wg_v = wg[l].rearrange("(k p) f -> p k f", p=128)
            wu_v = wu[l].rearrange("(k p) f -> p k f", p=128)
            wd_v = wd[l].rearrange("(k p) f -> p k f", p=128)
            for j in range(11):
                srcs = [(8, 0, 256, wg_v[:, :, j * 256:(j + 1) * 256]),
                        (8, 256, 256, wu_v[:, :, j * 256:(j + 1) * 256])]
                slabs[("gu%d" % l, j)] = conv_slab(srcs, 512)
            for j in range(8):
                slabs[("dn%d" % l, j)] = conv_slab([(NF, 0, 128, wd_v[:, :, j * 128:(j + 1) * 128])], 128)


        ffn_slabs(0)
        for j, c0 in enumerate((0, 1024, 512)):
            slabs[("in1", j)] = conv_slab([(8, 0, 512, w_in1_v[:, :, c0:c0 + 512])], 512)
        for j in range(2):
            slabs[("out1", j)] = conv_slab([(8, 0, 512, w_out1_v[:, :, j * 512:(j + 1) * 512])], 512)
        ffn_slabs(1)
        assert slab_no["i"] == NSLAB
        slab_elems = {}
        for (nm, j), idx in slabs.items():
            if nm.startswith("in0"):
                slab_elems[idx] = 8 * (512 if j < 3 else 256)
            elif nm.startswith("dn"):
                slab_elems[idx] = NF * 128
            else:
                slab_elems[idx] = 8 * 512
        order = [("in0", j) for j in range(4)] + [("out0", j) for j in range(2)] + \
                [("gu0", j) for j in range(11)] + [("dn0", j) for j in range(8)] + \
                [("in1", j) for j in range(3)] + [("out1", j) for j in range(2)] + \
                [("gu1", j) for j in range(11)] + [("dn1", j) for j in range(8)]
        seq_slabs = [slabs[k] for k in order]
        total_uses = ntiles * NSLAB
        wstate = {"next_load": NSLAB, "next_use": 0, "next_conv": 0}

        def issue_load():
            q = wstate["next_load"]
            if q >= total_uses:
                return
            wstate["next_load"] = q + 1
            idx = seq_slabs[q % NSLAB]
            n = slab_elems[idx]
            dma(ring[:, q % RING, 0:n], wscr[idx, :, 0:n], q="sp")

        def issue_conv():
            q = wstate["next_conv"]
            wstate["next_conv"] = q + 1
            idx = seq_slabs[q]
            srcs, width = slab_src[idx]
            kk = srcs[0][0]
            dst = ring[:, q % 2, :]
            k0 = 0
            for hf in range(2):
                nk = (kk + 1) // 2 if hf == 0 else kk - (kk + 1) // 2
                stg = ring[:, 2 + hf, :].bitcast(F32)
                for (kk_, c0, ncol, src) in srcs:
                    d_ = stg[:, 0:nk * width].rearrange("p (k c) -> p k c", c=width)[:, :, c0:c0 + ncol]
                    dma(d_, src[:, k0:k0 + nk, :], q="sp", rd_track=False)
                if hf == 0:
                    cp(dst[:, k0 * width:(k0 + nk) * width], stg[:, 0:nk * width], eng="pool")
                else:
                    wstate["pend"] = (q, dst[:, k0 * width:(k0 + nk) * width], stg[:, 0:nk * width], idx, dst)
                k0 += nk

        def finish_conv(q):
            pq, d_, s_, idx, dst = wstate.pop("pend")
            assert pq == q
            cp(d_, s_, eng="dve")
            n = slab_elems[idx]
            dma(wscr[idx, :, 0:n], dst[:, 0:n], q="pool")

        def use_slab(key):
            q = wstate["next_use"]
            wstate["next_use"] = q + 1
            assert seq_slabs[q % NSLAB] == slabs[key], (key, q)
            if q < NSLAB:
                if q == 0:
                    issue_conv()
                finish_conv(q)
                if q + 1 < NSLAB:
                    issue_conv()
                if q >= NSLAB - 1:
                    while wstate["next_load"] < min(q + RING, total_uses):
                        issue_load()
                return ring[:, q % 2, :]
            while wstate["next_load"] < min(q + RING, total_uses):
                issue_load()
            return ring[:, q % RING, :]

        def load_x(ti):
            s_, st_ = divmod(ti, nst_seq)
            for b in range(NB):
                t0 = st_ * ST + b * 128
                dma(stg_in[b][:], x[s_, t0:t0 + 128, :], q="pool", rd_track=False)

        class StatAcc:
            def __init__(self, bufs):
                self.bufs = bufs
                self.n = 0
                self.pend = None

            def _flush(self, last):
                if self.pend is not None:
                    i, b = self.pend
                    mm(bank(B_ST), ones_b[:], b, i == 0, last)
                    self.pend = None

            def add(self, src):
                b = self.bufs[self.n % 2]
                act(b, src, AF.Square)
                self._flush(False)
                self.pend = (self.n, b)
                self.n += 1

            def finish(self):
                assert self.n == KD
                self._flush(True)

        def norm_finish(hh, gcol0, hno, bst=None):
            bst = B_ST if bst is None else bst
            act(rstd_s, bank(bst), AF.Sqrt, bias=col_eps[:], scale=1.0 / D)
            recip(bank(bst), rstd_s)
            for k in range(KD):
                stt(hno[:, k, :], hh[:, k, :], col(gcol0 + k), bank(bst), ALU.mult, ALU.mult)

        def proj_resid(hh, src, slab_keys, sqbufs):
            sa = StatAcc(sqbufs)
            for j, key in enumerate(slab_keys):
                w = use_slab(key).rearrange("p (k c) -> p k c", c=512)
                for cc in range(4):
                    c = j * 4 + cc
                    bk = nbank(POOL_A)
                    for k in range(KD):
                        mm(bank(bk), w[:, k, cc * 128:(cc + 1) * 128], src[:, k, :], k == 0, k == KD - 1)
                    tt(hh[:, c, :], bank(bk), hh[:, c, :], ALU.add)
                    sa.add(hh[:, c, :])
            sa.finish()

        def ffn_gu(hh, l, hnb, hook=None, pre_hook=None):
            pg.label = "F%d_norm" % l
            norm_finish(hh, C_GFFN0 if l == 0 else C_GFFN1, hnb)
            if pre_hook is not None:
                pre_hook()
            pg.label = "F%d_gu" % l
            for j in range(11):
                w = use_slab(("gu%d" % l, j)).rearrange("p (k c) -> p k c", c=512)
                for i in range(2):
                    f_ = 2 * j + i
                    bg = nbank(POOL_F)
                    for k in range(KD):
                        mm(bank(bg), w[:, k, i * 128:i * 128 + 128], hnb[:, k, :], k == 0, k == KD - 1)
                    bu = nbank(POOL_F)
                    for k in range(KD):
                        mm(bank(bu), w[:, k, 256 + i * 128:256 + i * 128 + 128], hnb[:, k, :], k == 0, k == KD - 1)
                    sg_ = sgt[f_ % 4]
                    act(sg_, bank(bg), AF.Silu)
                    tt(actb[:, f_, :], bank(bu), sg_, ALU.mult)
                if hook is not None and j == 7:
                    hook()
                    pg.label = "F%d_gu" % l

        def ffn_dn(hh, l):
            pg.label = "F%d_dn" % l
            sa = StatAcc(sqrot_f)
            for j in range(8):
                w = use_slab(("dn%d" % l, j)).rearrange("p (f c) -> p f c", c=128)
                bk = nbank(POOL_F)
                for f_ in range(NF):
                    mm(bank(bk), w[:, f_, :], actb[:, f_, :], f_ == 0, f_ == NF - 1)
                tt(hh[:, j, :], bank(bk), hh[:, j, :], ALU.add)
                sa.add(hh[:, j, :])
            sa.finish()

        def gen_diag(cc):
            slots = []
            for j in range(31):
                dslot = dstate["i"] % NDIAG
                dstate["i"] += 1
                ts(diagbuf[:, dslot, :], identb[:], col(C_CONVW + cc * 31 + j), ALU.mult, 0.0, ALU.add,
                   eng="pool")
                slots.append(dslot)
            return slots

        def A_pre(ti, hh, hnb, bpool):
            pg.label = "A1_xT"
            pend = None

            def blk_stats(b):
                for k in range(KD):
                    mm(psum[:, B_O, b * 128:(b + 1) * 128], ones_b[:], sqblk[b % 2][:, k, :], k == 0, k == KD - 1)

            for b in range(NB):
                if b >= 2:
                    blk_stats(b - 2)
                for half in range(2):
                    bk = nbank(bpool)
                    for kk in range(4):
                        k = half * 4 + kk
                        tr(psum[:, bk, kk * 128:(kk + 1) * 128], stg_in[b][:, k * 128:(k + 1) * 128], ident[:])
                    o = hh[:, half * 4:half * 4 + 4, b * 128:(b + 1) * 128]
                    i_ = bank(bk).rearrange("p (k t) -> p k t", t=128)
                    if b >= 2:
                        cp(o, i_, eng="dve")
                    else:
                        cp(o, i_, eng="act")
                    act(sqblk[b % 2][:, half * 4:half * 4 + 4, :], o, AF.Square)
            blk_stats(2)
            blk_stats(3)
            if dbg and "h_in" in dbg_out and ti == 0:
                dma(dbg_out["h_in"], hh[:], q="sp")
            pg.label = "A2_norm"
            norm_finish(hh, C_GMIX0, hnb, bst=B_O)

        def A_main(ti, hh, hnb):
            s_, st_ = divmod(ti, nst_seq)
            first = st_ == 0
            dslots = {0: gen_diag(0)}
            pg.label = "A4_inproj"
            if first:
                memset(glu[:, :, 0:30], 0.0)
            else:
                cp(kT[:, 0:128], kT[:, ST:ST + 128], eng="pool")
                cp(vtok[:, 0, :], vtok[:, NB, :], eng="pool")
                cp(glu[:, :, 0:30], glu[:, :, ST:ST + 30], eng="pool")
            w = use_slab(("in0", 0)).rearrange("p (k c) -> p k c", c=512)
            for c in range(4):
                bk = nbank(POOL_A)
                for k in range(KD):
                    mm(bank(bk), w[:, k, c * 128:(c + 1) * 128], hnb[:, k, :], k == 0, k == KD - 1)
                act(qT[:, c, :], bank(bk), AF.Identity, bias=col(C_BIN + c))
            w1 = use_slab(("in0", 1)).rearrange("p (k c) -> p k c", c=512)
            bk = nbank(POOL_A)
            for k in range(KD):
                mm(bank(bk), w1[:, k, 0:128], hnb[:, k, :], k == 0, k == KD - 1)
            act(kT[:, 128:128 + ST], bank(bk), AF.Identity, bias=col(C_BIN + 4))
            bk = nbank(POOL_A)
            for b in range(NB):
                o = psum[:, bk, b * 128:(b + 1) * 128]
                for k in range(KD):
                    mm(o, hnb[:, k, b * 128:(b + 1) * 128], w1[:, k, 128:256], k == 0, False)
                mm(o, ones_b[0:1, :], rows_b[0:1, R_BV:R_BV + 128], False, True)
            cp(vtok[:, 1:1 + NB, :], bank(bk).rearrange("p (b c) -> p b c", c=128), eng="act")

            def glu_pair(cc, wa, wg_):
                ba = nbank(POOL_A)
                for k in range(KD):
                    mm(bank(ba), wa[k], hnb[:, k, :], k == 0, k == KD - 1)
                bg = nbank(POOL_A)
                for k in range(KD):
                    mm(bank(bg), wg_[k], hnb[:, k, :], k == 0, k == KD - 1)
                sg_ = sig[cc % 2]
                act(sg_, bank(bg), AF.Sigmoid, bias=col(C_BIN + 7 + 2 * cc))
                stt(glu[:, cc, 30:30 + ST], bank(ba), col(C_BIN + 6 + 2 * cc), sg_, ALU.add, ALU.mult)

            glu_pair(0, [w1[:, k, 256:384] for k in range(KD)], [w1[:, k, 384:512] for k in range(KD)])

            wst_ = {}

            def f_glu1():
                w2 = use_slab(("in0", 2)).rearrange("p (k c) -> p k c", c=512)
                wst_["w2"] = w2
                glu_pair(1, [w2[:, k, 0:128] for k in range(KD)], [w2[:, k, 128:256] for k in range(KD)])

            def f_glu2():
                w2 = wst_["w2"]
                glu_pair(2, [w2[:, k, 256:384] for k in range(KD)], [w2[:, k, 384:512] for k in range(KD)])

            def f_glu3():
                w3 = use_slab(("in0", 3)).rearrange("p (k c) -> p k c", c=256)
                glu_pair(3, [w3[:, k, 0:128] for k in range(KD)], [w3[:, k, 128:256] for k in range(KD)])

            cbank = {}

            def f_conv(cc, half):
                def f():
                    lab = pg.label
                    pg.label = "A6_conv"
                    if half == 0:
                        cbank[cc] = nbank(POOL_A)
                        if cc + 1 < 4:
                            dslots[cc + 1] = gen_diag(cc + 1)
                    bk_ = cbank[cc]
                    for j in (range(0, 16) if half == 0 else range(16, 31)):
                        mm(bank(bk_), diagbuf[:, dslots[cc][j], :], glu[:, cc, j:j + ST], j == 0, j == 30)
                    if half == 1:
                        act(cv[:, cc, :], bank(bk_), AF.Identity, bias=col(C_CONVB + cc))
                        act(cvb[:, cc, :], bank(bk_), AF.Identity, bias=col(C_CONVB + cc))
                        act(cvq[:, cc, :], cv[:, cc, :], AF.Square)
                    pg.label = lab
                return f

            def f_ln():
                lab = pg.label
                pg.label = "A6_conv"
                bm = nbank(POOL_A)
                for cc in range(4):
                    mm(bank(bm), onesm_b[:], cvb[:, cc, :], cc == 0, cc == 3)
                for cc in range(4):
                    mm(bank(B_ST), onesm_b[:], cvq[:, cc, :], cc == 0, cc == 3)
                m2 = lnt[0]
                act(m2, bank(bm), AF.Square)
                tt(m2, bank(B_ST), m2, ALU.subtract)
                act(m2, m2, AF.Sqrt, bias=col_eps[:], scale=1.0)
                recip(bank(B_ST), m2)
                for cc in range(4):
                    d_ = lnt[1 + cc % 2]
                    tt(d_, cv[:, cc, :], bank(bm), ALU.subtract)
                    stt(d_, d_, col(C_CLNG + cc), bank(B_ST), ALU.mult, ALU.mult)
                    act(cat[:, 4 + cc, :], d_, AF.Silu, bias=col(C_CLNB + cc))
                pg.label = lab

            fillers = [f_glu1, f_glu2, f_glu3]
            for cc in range(4):
                fillers += [f_conv(cc, 0), f_conv(cc, 1)]
            fillers.append(f_ln)

            def filler():
                if fillers:
                    fillers.pop(0)()

            pg.label = "A5_attn"
            S4 = psum[:, B_S0:B_S0 + 2, :].rearrange("p a (g s) -> p (a g) s", s=256)

            def att_S(p):
                b, kv = divmod(p, 2)
                ph = kv * 64
                mi = 1 if (first and b == 0) else 0
                mk = mask2[:, mi:mi + 1, :].broadcast_to([128, 4, 256])
                for g in range(4):
                    mm(S4[:, g, :], qT[ph:ph + 64, g, b * 128:(b + 1) * 128],
                       kT[ph:ph + 64, b * 128:b * 128 + 256], True, True)
                sm_ = Sm[p % 2]
                tt(sm_[:, :, :], S4[:, :, :], mk, ALU.add)
                so = (p % 2) * 32
                mx = small[:, so + 0:so + 4]
                negm = small[:, so + 4:so + 8]
                dd = small[:, so + 8:so + 12]
                es_ = small[:, so + 12:so + 16]
                bo = (b % 2) * 16 + kv * 4
                ll = st8[:, bo:bo + 4]
                rinv = st8[:, 32 + bo:32 + bo + 4]
                reduce_max(mx, sm_[:, :, :])
                stt(negm, mx, -0.125, nsink[:, kv * 4:kv * 4 + 4], ALU.mult, ALU.min)
                tt(dd, cols[:, C_SINK + kv * 4:C_SINK + kv * 4 + 4], negm, ALU.add)
                for g in range(4):
                    act(Pb[p % 2][:, g, :], sm_[:, g, :], AF.Exp, bias=negm[:, g:g + 1], scale=0.125,
                        accum=ll[:, g:g + 1])
                act(es_, dd, AF.Exp)
                tt(ll, ll, es_, ALU.add)
                recip(rinv, ll)

            def att_T(p):
                PTp = bank(B_PT).bitcast(BF).rearrange("p (a g q) -> p a g q", g=4, q=128)
                for a in range(2):
                    for g in range(4):
                        tr(PTp[:, a, g, :], Pb[p % 2][:, g, a * 128:(a + 1) * 128], identb[:])
                cp(PT[p % 2][:], bank(B_PT).bitcast(BF).rearrange("p (a c) -> p a c", c=512), eng="act")

            def att_PV(p):
                b, kv = divmod(p, 2)
                ph = kv * 64
                for g in range(4):
                    o = psum[:, B_O, g * 128 + ph:g * 128 + ph + 64]
                    for a in range(2):
                        mm(o, PT[p % 2][:, a, g * 128:(g + 1) * 128], vtok[:, b + a, ph:ph + 64], a == 0, a == 1)
                if kv == 1:
                    bo = (b % 2) * 16
                    rbc = st8[:, 32 + bo:32 + bo + 8].rearrange("p (kv g) -> p g kv", g=4) \
                        .unsqueeze(3).broadcast_to([128, 4, 2, 64])
                    tt(Osb.rearrange("p (g kv d) -> p g kv d", kv=2, d=64),
                       bank(B_O).rearrange("p (g kv d) -> p g kv d", kv=2, d=64), rbc, ALU.mult)
                    bk_ = nbank(POOL_A)
                    ov = bank(bk_).bitcast(BF)[:, 0:512].rearrange("p (g q) -> p g q", q=128)
                    for g in range(4):
                        tr(ov[:, g, :], Osb[:, g * 128:(g + 1) * 128], identb[:])
                    cp(cat[:, 0:4, b * 128:(b + 1) * 128], ov, eng="act")

            NP = 2 * NB
            att_S(0)
            filler()
            for p in range(NP):
                if p + 1 < NP:
                    att_S(p + 1)
                filler()
                att_T(p)
                if p % 2 == 1:
                    filler()
                if p > 0:
                    att_PV(p - 1)
            while fillers:
                filler()
            att_PV(NP - 1)
            pg.label = "A7_out"
            proj_resid(hh, cat, [("out0", 0), ("out0", 1)], sqrot_a)

        def phase_C(ti, hh, hnb):
            s_, st_ = divmod(ti, nst_seq)
            first = st_ == 0
            pg.label = "C1_norm"
            norm_finish(hh, C_GMIX1, hnb)
            pg.label = "C2_in"
            if not first:
                cp(zptok[:, 0, :], zptok[:, NB, :], eng="pool")
            w = use_slab(("in1", 0)).rearrange("p (k c) -> p k c", c=512)
            for b in range(NB):
                bk = nbank(POOL_A)
                for k in range(KD):
                    mm(bank(bk), hnb[:, k, b * 128:(b + 1) * 128], w[:, k, :], k == 0, k == KD - 1)
                cp(zptok[:, 1 + b, :], bank(bk), eng=("act" if b % 2 else "dve"))
            wv = use_slab(("in1", 1)).rearrange("p (k c) -> p k c", c=512)
            for b in range(NB):
                bk = nbank(POOL_A)
                for k in range(KD):
                    mm(bank(bk), hnb[:, k, b * 128:(b + 1) * 128], wv[:, k, :], k == 0, k == KD - 1)
                vg_ = vg[b % 2]
                vn_ = vn[b % 2]
                act(vg_, bank(bk), AF.Gelu_apprx_tanh)
                st6 = smallc[:, b * 16:b * 16 + 6]
                mv = smallc[:, b * 16 + 8:b * 16 + 10]
                sd = smallc[:, b * 16 + 10:b * 16 + 11]
                pg.add("dve", lambda e, o=st6, i_=vg_: e.bn_stats(out=o, in_=i_), reads=[vg_], writes=[st6])
                pg.add("dve", lambda e, o=mv, i_=st6: e.bn_aggr(out=o, in_=i_), reads=[st6], writes=[mv])
                act(sd, mv[:, 1:2], AF.Sqrt, bias=col_eps[:], scale=1.0)
                recip(sd, sd)
                ts(vn_, vg_, mv[:, 0:1], ALU.subtract, sd, ALU.mult)
                tt(vn_, vn_, bcs[:, 0, :], ALU.mult, eng="pool")
                tt(vln[:, b, :], vn_, bcs[:, 1, :], ALU.add, eng="pool")
            w = use_slab(("in1", 2)).rearrange("p (k c) -> p k c", c=512)
            for c in range(4):
                bk = nbank(POOL_A)
                for k in range(KD):
                    mm(bank(bk), w[:, k, c * 128:(c + 1) * 128], hnb[:, k, :], k == 0, k == KD - 1)
                act(uT[:, c, :], bank(bk), AF.Gelu_apprx_tanh)
            pg.label = "C3_pool"
            for g in range(4):
                bk = nbank(POOL_A)
                for b in range(NB):
                    o = psum[:, bk, b * 128:(b + 1) * 128]
                    if first and b == 0:
                        mm(o, zptok[:, 1 + b, g * 128:(g + 1) * 128], poolm[:, 8 + g, :], True, True)
                    else:
                        mm(o, zptok[:, 1 + b, g * 128:(g + 1) * 128], poolm[:, g, :], True, False)
                        mm(o, zptok[:, b, g * 128:(g + 1) * 128], poolm[:, 4 + g, :], False, True)
                cp(pooled[:, g, :], bank(bk), eng=("act" if g % 2 else "dve"))
            for g in range(4):
                bk2 = nbank(POOL_A)
                mm(bank(bk2), wpool_b[:, g, :], pooled[:, g, :], True, True)
                act(cat[:, g, :], bank(bk2), AF.Identity, scale=col(C_PSCALE + g))
            pg.label = "C4_sgu"
            for g in range(4):
                bk = nbank(POOL_A)
                for b in range(NB):
                    o = psum[:, bk, b * 128:(b + 1) * 128]
                    mm(o, vln[:, b, g * 128:(g + 1) * 128], wst_b[:, g, :], True, False)
                    mm(o, ones_b[0:1, :], rows_b[0:1, R_BS + g * 128:R_BS + (g + 1) * 128], False, True)
                tt(cat[:, 4 + g, :], bank(bk), uT[:, g, :], ALU.mult)
            pg.label = "C5_out"
            proj_resid(hh, cat, [("out1", 0), ("out1", 1)], sqrot_c)

        def E_pre(ti, hh):
            pg.label = "E_final"
            act(rstd_s, bank(B_ST), AF.Sqrt, bias=col_eps[:], scale=1.0 / D)
            recip(bank(B_ST), rstd_s)
            for k in range(KD):
                stt(hh[:, k, :], hh[:, k, :], col(C_GFIN + k), bank(B_ST), ALU.mult, ALU.mult)

        def E_post(ti, hh):
            s_, st_ = divmod(ti, nst_seq)
            pg.label = "E_final"
            for b in range(NB):
                so = stg_out[b % 2]
                for half in range(2):
                    bk = nbank(POOL_F)
                    for kk in range(4):
                        k = half * 4 + kk
                        tr(psum[:, bk, kk * 128:(kk + 1) * 128], hh[:, k, b * 128:(b + 1) * 128], ident[:])
                    if half == 0:
                        cp(so[:, 0:512], bank(bk), eng="act")
                    else:
                        cp(so[:, 512:1024], bank(bk), eng="dve")
                t0 = st_ * ST + b * 128
                dma(out[s_, t0:t0 + 128, :], so[:], q="pool", is_out=True)

        load_x(0)
        A_pre(0, h[0], hn[0], POOL_A)
        for ti in range(ntiles):
            hh = h[ti % 2]
            if ti + 1 < ntiles:
                load_x(ti + 1)
            A_main(ti, hh, hn[0])
            if dbg and "h_a" in dbg_out and ti == 0:
                dma(dbg_out["h_a"], hh[:], q="sp")
            ffn_gu(hh, 0, hn[1], pre_hook=((lambda t=ti: E_post(t - 1, h[(t - 1) % 2])) if ti > 0 else None))
            ffn_dn(hh, 0)
            if dbg and "h_b" in dbg_out and ti == 0:
                dma(dbg_out["h_b"], hh[:], q="sp")
            phase_C(ti, hh, hn[0])
            if dbg and "h_c" in dbg_out and ti == 0:
                dma(dbg_out["h_c"], hh[:], q="sp")
            ffn_gu(hh, 1, hn[1],
                   pre_hook=((lambda t=ti: A_pre(t + 1, h[(t + 1) % 2], hn[0], [0, 1, 2, 3, 4, 5])) if ti + 1 < ntiles else None))
            ffn_dn(hh, 1)
            E_pre(ti, hh)
        E_post(ntiles - 1, h[(ntiles - 1) % 2])
        pg.emit_all(block, sems, dsems)
    return nc, pg


_CACHE = {}


def kernel(**inputs):
    xs = np.asarray(inputs["x"])
    B, S, _ = xs.shape
    nseq = B // NCORES
    maps = prep_inputs(inputs, NCORES, nseq)
    key = (nseq, S)
    if key not in _CACHE:
        _CACHE[key] = build(nseq, S)[0]
    nc = _CACHE[key]
    res = run_bass_kernel_spmd(nc, maps, core_ids=list(range(NCORES)))
    outs = [np.asarray(r["out"]) for r in res.results]
    return np.concatenate(outs, axis=0).astype(np.float32)
```

```python
import numpy as np
import concourse.bass as bass
import concourse.mybir as mybir
from concourse.bass_utils import run_bass_kernel_spmd

F32 = mybir.dt.float32
BF = mybir.dt.bfloat16
AF = mybir.ActivationFunctionType
ALU = mybir.AluOpType
AX = mybir.AxisListType

NCORES = 8
D = 1024
KD = 8
DFF = 2816
NF = 22
ST = 512
NB = 4
EPS = 1e-5
SLAB = 4096
NSLAB = 49
RING = 4
NDSEM = 8

C_GMIX0, C_GMIX1, C_GFFN0, C_GFFN1, C_GFIN = 0, 8, 16, 24, 32
C_BIN = 40
C_CONVW = 54
C_CONVB = 178
C_CLNG = 182
C_CLNB = 186
C_PSCALE = 190
C_SINK = 194
NCOL = 202
K_IDENT = 0
K_MASK = 128
K_TRIL = 640
K_POOL = 768
NCST = 768 + 1536
R_BV = 0
R_BS = 128
NROW = 640


def _esize(dt):
    return 2 if dt == BF else 4


class Op:
    __slots__ = ("id", "stream", "emit", "deps", "sig", "cnt", "dma", "dsem", "dval", "label")


class Prog:
    STREAMS = ("pe", "act", "dve", "pool", "sp")

    def __init__(self):
        self.nc = bass.Bass("TRN2", target_bir_lowering=False)
        self.ops = []
        self.streams = {s: [] for s in self.STREAMS}
        self.track = {}
        self.ndma = {s: 0 for s in self.STREAMS}
        self.dma_ops = {s: [] for s in self.STREAMS}
        self.out_dmas = []
        self.label = ""

    @staticmethod
    def _rng(ap):
        t = ap.tensor
        es = _esize(ap.dtype)
        dims = ap.ap
        space = str(ap.space)
        if "DRAM" in space.upper() or "HBM" in space.upper():
            lo = ap.offset
            ext = 0
            for (stp, cnt) in dims:
                ext += (cnt - 1) * abs(stp)
            return (t.name, lo * es, (lo + ext + 1) * es, 0, 128)
        pstep, pcnt = dims[0]
        p0 = ap.start_partition()
        lo = ap.offset - p0 * pstep
        ext = 0
        for (stp, cnt) in dims[1:]:
            ext += (cnt - 1) * abs(stp)
        if "PSUM" in space.upper():
            b0 = (lo * es) // 2048
            b1 = ((lo + ext + 1) * es + 2047) // 2048
            return ("~psum", b0 * 2048, b1 * 2048, 0, 128)
        return (t.name, lo * es, (lo + ext + 1) * es, p0, p0 + pcnt)

    def add(self, stream, emit, reads=(), writes=(), dma=False, is_out=False):
        op = Op()
        op.id = len(self.ops)
        op.stream = stream
        op.emit = emit
        op.deps = set()
        op.sig = False
        op.cnt = 0
        op.dma = dma
        op.dsem = None
        op.dval = 0
        op.label = self.label
        if dma:
            n = self.ndma[stream]
            self.ndma[stream] = n + 1
            op.dsem = n % NDSEM
            op.dval = 16 * (n // NDSEM + 1)
            if n >= NDSEM:
                op.deps.add(self.dma_ops[stream][n - NDSEM].id)
            self.dma_ops[stream].append(op)
            if is_out:
                self.out_dmas.append(op)
        rr = [self._rng(a) for a in reads if a is not None]
        ww = [self._rng(a) for a in writes if a is not None]
        for (name, lo, hi, p0, p1) in rr:
            excl = name == "~psum"
            for rec in self.track.get(name, ()):
                if (rec[5] or (excl and rec[6][0] != stream)) and rec[0] < hi and lo < rec[1] \
                        and rec[2] < p1 and p0 < rec[3]:
                    op.deps.add(rec[4])
        for (name, lo, hi, p0, p1) in ww:
            for rec in self.track.get(name, ()):
                if rec[0] < hi and lo < rec[1] and rec[2] < p1 and p0 < rec[3]:
                    op.deps.add(rec[4])
        op.deps.discard(op.id)
        ekey = (stream, dma and op.id)
        for (name, lo, hi, p0, p1) in ww:
            lst = self.track.setdefault(name, [])
            lst[:] = [r for r in lst if not (lo <= r[0] and r[1] <= hi and p0 <= r[2] and r[3] <= p1)]
            lst.append((lo, hi, p0, p1, op.id, True, ekey))
        for (name, lo, hi, p0, p1) in rr:
            lst = self.track.setdefault(name, [])
            if not dma:
                lst[:] = [r for r in lst if not ((not r[5]) and r[6] == ekey and lo <= r[0] and r[1] <= hi
                                                 and p0 <= r[2] and r[3] <= p1)]
            lst.append((lo, hi, p0, p1, op.id, False, ekey))
        self.ops.append(op)
        self.streams[stream].append(op)
        return op

    def emit_all(self, block, sems, dsems):
        ops = self.ops
        for op in ops:
            for d in op.deps:
                dop = ops[d]
                if not dop.dma:
                    if dop.stream == "pe" and op.stream == "pe" and not op.dma:
                        continue
                    dop.sig = True
        for s in self.STREAMS:
            c = 0
            for op in self.streams[s]:
                if (not op.dma) and op.sig:
                    c += 1
                    op.cnt = c
        nwaits = {s: 0 for s in self.STREAMS}

        def run_stream(s, eng):
            seen = {}
            for op in self.streams[s]:
                waits = {}
                for d in op.deps:
                    dop = ops[d]
                    if dop.dma:
                        key = ("d", dop.stream, dop.dsem)
                        val = dop.dval
                    else:
                        if dop.stream == "pe" and s == "pe" and not op.dma:
                            continue
                        key = ("c", dop.stream)
                        val = dop.cnt
                    if waits.get(key, 0) < val:
                        waits[key] = val
                for key, val in waits.items():
                    if seen.get(key, 0) >= val:
                        continue
                    seen[key] = val
                    sem = dsems[key[1]][key[2]] if key[0] == "d" else sems[key[1]]
                    eng.wait_ge(sem, val)
                    nwaits[s] += 1
                ins = op.emit(eng)
                if op.dma:
                    ins.then_inc(dsems[s][op.dsem], 16)
                elif op.sig:
                    ins.then_inc(sems[s], 1)
            if s == "sp":
                fin = {}
                for op in self.out_dmas:
                    key = (op.stream, op.dsem)
                    fin[key] = max(fin.get(key, 0), op.dval)
                for (st_, di), val in fin.items():
                    eng.wait_ge(dsems[st_][di], val)

        @block.tensor
        def _(e):
            run_stream("pe", e)

        @block.scalar
        def _(e):
            run_stream("act", e)

        @block.vector
        def _(e):
            run_stream("dve", e)

        @block.gpsimd
        def _(e):
            run_stream("pool", e)

        @block.sync
        def _(e):
            run_stream("sp", e)

        self.nwaits = nwaits


def perm_q():
    idx = np.zeros(512, dtype=np.int64)
    for c in range(4):
        for half in range(2):
            for d in range(64):
                idx[c * 128 + half * 64 + d] = (half * 4 + c) * 64 + d
    return idx


def perm_in0():
    pq = perm_q()
    cols = list(pq) + list(range(512, 768))
    for cc in range(4):
        cols += list(range(768 + cc * 128, 768 + (cc + 1) * 128))
        cols += list(range(1280 + cc * 128, 1280 + (cc + 1) * 128))
    return np.array(cols, dtype=np.int64)


def host_consts():
    cst = np.zeros((128, NCST), dtype=np.float32)
    cst[:, K_IDENT:K_IDENT + 128] = np.eye(128, dtype=np.float32)
    qi = np.arange(128)[:, None]
    r = np.arange(256)[None, :]
    dist = qi + 128 - r
    band = (dist >= 0) & (dist < 128)
    NEG = -30000.0
    cst[:, K_MASK:K_MASK + 256] = np.where(band, 0.0, NEG)
    first = band & (r >= 128)
    cst[:, K_MASK + 256:K_MASK + 512] = np.where(first, 0.0, NEG)
    s = np.arange(128)[:, None]
    t = np.arange(128)[None, :]
    cst[:, K_TRIL:K_TRIL + 128] = (s <= t).astype(np.float32)
    for g, w in enumerate((2, 4, 8, 16)):
        cur = np.where((t - s >= 0) & (t - s < w), 1.0 / w, 0.0) - (s == t)
        prev = np.where((t + 128 - s) < w, 1.0 / w, 0.0)
        cnt = np.minimum(t + 1, w).astype(np.float64)
        fst = np.where((t - s >= 0) & (t - s < w), 1.0 / cnt, 0.0) - (s == t)
        cst[:, K_POOL + g * 128:K_POOL + (g + 1) * 128] = cur
        cst[:, K_POOL + (4 + g) * 128:K_POOL + (5 + g) * 128] = prev
        cst[:, K_POOL + (8 + g) * 128:K_POOL + (9 + g) * 128] = fst
    return cst


def prep_inputs(inp, ncores, nseq):
    f = lambda a: np.ascontiguousarray(np.asarray(a, dtype=np.float32))
    x = f(inp["x"])
    pin = perm_in0()
    pq = perm_q()
    a_w_in = f(inp["a_w_in"])[0][:, pin]
    a_b_in = f(inp["a_b_in"])[0][pin]
    a_w_out = f(inp["a_w_out"])[0]
    a_w_out = np.concatenate([a_w_out[pq], a_w_out[512:]], axis=0)
    cols = np.zeros((128, NCOL), dtype=np.float32)
    colv = lambda v: np.asarray(v, dtype=np.float32).reshape(-1, 128).T
    cols[:, C_GMIX0:C_GMIX0 + 8] = colv(inp["mix_norm"][0])
    cols[:, C_GMIX1:C_GMIX1 + 8] = colv(inp["mix_norm"][1])
    cols[:, C_GFFN0:C_GFFN0 + 8] = colv(inp["ffn_norm"][0])
    cols[:, C_GFFN1:C_GFFN1 + 8] = colv(inp["ffn_norm"][1])
    cols[:, C_GFIN:C_GFIN + 8] = colv(inp["final_norm"])
    cols[:, C_BIN:C_BIN + 14] = colv(a_b_in)
    cw = f(inp["a_conv_w"])[0]
    cols[:, C_CONVW:C_CONVW + 124] = cw.T.reshape(4, 128, 31).transpose(1, 0, 2).reshape(128, 124)
    cols[:, C_CONVB:C_CONVB + 4] = colv(inp["a_conv_b"][0])
    cols[:, C_CLNG:C_CLNG + 4] = colv(inp["a_cln_g"][0])
    cols[:, C_CLNB:C_CLNB + 4] = colv(inp["a_cln_b"][0])
    cols[:, C_PSCALE:C_PSCALE + 4] = colv(inp["c_pool_scale"][0])
    cols[:, C_SINK:C_SINK + 8] = np.broadcast_to(f(inp["a_sinks"])[0][None, :], (128, 8))
    rows = np.zeros((1, NROW), dtype=np.float32)
    rows[0, R_BV:R_BV + 128] = a_b_in[640:768]
    rows[0, R_BS:R_BS + 512] = f(inp["c_b_s"])[0].reshape(512)
    bc = np.zeros((128, 2, 512), dtype=np.float32)
    bc[:, 0, :] = np.broadcast_to(f(inp["c_sln_g"])[0][None, :], (128, 512))
    bc[:, 1, :] = np.broadcast_to(f(inp["c_sln_b"])[0][None, :], (128, 512))
    wpool = np.ascontiguousarray(f(inp["c_w_pool"])[0].transpose(1, 0, 2))
    wst = np.ascontiguousarray(f(inp["c_w_s"])[0].transpose(2, 0, 1))
    shared = {
        "w_in0": np.ascontiguousarray(a_w_in), "w_out0": np.ascontiguousarray(a_w_out),
        "w_in1": f(inp["c_w_in"])[0], "w_out1": f(inp["c_w_out"])[0],
        "wg": f(inp["ffn_w_gate"]), "wu": f(inp["ffn_w_up"]), "wd": f(inp["ffn_w_down"]),
        "cols": cols, "rows": rows, "bc": bc, "wpool": wpool, "wst": wst, "cst": host_consts(),
    }
    maps = []
    for c in range(ncores):
        m = dict(shared)
        m["x"] = np.ascontiguousarray(x[c * nseq:(c + 1) * nseq])
        maps.append(m)
    return maps


def build(nseq, seqlen, dbg=None):
    pg = Prog()
    nc = pg.nc
    nst_seq = seqlen // ST
    ntiles = nseq * nst_seq
    dram_in = lambda name, shape: nc.dram_tensor(name, shape, F32, kind="ExternalInput").ap()
    x = dram_in("x", [nseq, seqlen, D])
    w_in0 = dram_in("w_in0", [D, 1792])
    w_out0 = dram_in("w_out0", [D, D])
    w_in1 = dram_in("w_in1", [D, 1536])
    w_out1 = dram_in("w_out1", [D, D])
    wg = dram_in("wg", [2, D, DFF])
    wu = dram_in("wu", [2, D, DFF])
    wd = dram_in("wd", [2, DFF, D])
    cols_d = dram_in("cols", [128, NCOL])
    rows_d = dram_in("rows", [1, NROW])
    bc_d = dram_in("bc", [128, 2, 512])
    wpool_d = dram_in("wpool", [128, 4, 128])
    wst_d = dram_in("wst", [128, 4, 128])
    cst_d = dram_in("cst", [128, NCST])
    out = nc.dram_tensor("out", [nseq, seqlen, D], F32, kind="ExternalOutput").ap()
    wscr = nc.dram_tensor("wscr", [NSLAB, 128, SLAB], BF).ap()
    dbg_out = {}
    if dbg:
        for name, shape in dbg.items():
            dbg_out[name] = nc.dram_tensor("dbg_" + name, shape, F32, kind="ExternalOutput").ap()

    import contextlib
    es = contextlib.ExitStack()
    with es:
        sb = lambda name, shape, dt: es.enter_context(nc.sbuf_tensor(name, shape, dt))
        cols = sb("cols_s", [128, NCOL], F32)
        nsink = sb("nsink", [128, 8], F32)
        ident = sb("ident", [128, 128], F32)
        identb = sb("identb", [128, 128], BF)
        mask2 = sb("mask2", [128, 2, 256], F32)
        ones_b = sb("ones_b", [128, 128], BF)
        onesm_b = sb("onesm_b", [128, 128], BF)
        rows_b = sb("rows_b", [1, NROW], BF)
        bcs = sb("bcs", [128, 2, 512], F32)
        poolm = sb("poolm", [128, 12, 128], BF)
        wpool_b = sb("wpool_b", [128, 4, 128], BF)
        wst_b = sb("wst_b", [128, 4, 128], BF)
        NDIAG = 64
        diagbuf = sb("diagbuf", [128, NDIAG, 128], BF)
        col_eps = sb("col_eps", [128, 1], F32)
        ring = sb("ring", [128, RING, SLAB], BF)
        h = [sb("h%d" % i, [128, KD, ST], F32) for i in range(2)]
        hn = [sb("hn%d" % i, [128, KD, ST], BF) for i in range(2)]
        stg_in = [sb("stgi%d" % i, [128, D], F32) for i in range(4)]
        stg_out = [sb("stgo%d" % i, [128, D], F32) for i in range(2)]
        kT = sb("kT", [128, 128 + ST], BF)
        vtok = sb("vtok", [128, 1 + NB, 128], BF)
        glu = sb("glu", [128, 4, 30 + ST], BF)
        zptok = sb("zptok", [128, 1 + NB, 512], BF)
        qT = sb("qT", [128, 4, ST], BF)
        cat = sb("cat", [128, KD, ST], BF)
        arena = sb("arena", [128, 24 * 1024], BF)
        psum = es.enter_context(nc.psum_tensor("psum", [128, 8, 512], F32))
        sems = {s: es.enter_context(nc.semaphore("sem_" + s)) for s in Prog.STREAMS}
        dsems = {s: [es.enter_context(nc.semaphore("dsem_%s%d" % (s, i))) for i in range(NDSEM)]
                 for s in ("sp", "pool", "act")}
        block = es.enter_context(nc.Block())

        def aview(off_b, nelem, dt):
            o = off_b // 2
            n = nelem * _esize(dt) // 2
            v = arena[:, o:o + n]
            return v if dt == BF else v.bitcast(dt)

        KB = 1024
        sqrot_a = [aview(i * KB, 512, BF) for i in range(2)]
        sqblk = [aview(32 * KB + i * 2 * KB, 1024, BF).rearrange("p (k t) -> p k t", t=128) for i in range(2)]
        small = aview(6 * KB, 256, F32)
        rstd_s = aview(30 * KB, 512, F32)
        sig = [aview(10 * KB + i * 2 * KB, 512, F32) for i in range(2)]
        Sm = [aview(14 * KB + i * 4 * KB, 1024, F32).rearrange("p (g s) -> p g s", s=256) for i in range(2)]
        Osb = aview(44 * KB, 512, BF)
        st8 = aview(45 * KB, 256, F32)
        Pb = [aview(22 * KB + i * 2 * KB, 1024, BF).rearrange("p (g s) -> p g s", s=256) for i in range(2)]
        PT = [aview(o_ * KB, 1024, BF).rearrange("p (a c) -> p a c", c=512) for o_ in (26, 46)]
        cv = aview(28 * KB, 2048, F32).rearrange("p (c t) -> p c t", t=ST)
        cvb = aview(36 * KB, 2048, BF).rearrange("p (c t) -> p c t", t=ST)
        cvq = aview(40 * KB, 2048, BF).rearrange("p (c t) -> p c t", t=ST)
        lnt = [aview(i * 2 * KB, 512, F32) for i in range(3)]
        actb = aview(0, NF * ST, BF).rearrange("p (f t) -> p f t", t=ST)
        sgt = [aview(22 * KB + i * KB, 512, BF) for i in range(4)]
        sqrot_f = [aview(26 * KB + i * KB, 512, BF) for i in range(2)]
        uT = aview(0, 2048, BF).rearrange("p (c t) -> p c t", t=ST)
        vg = [aview(4 * KB + i * 2 * KB, 512, F32) for i in range(2)]
        vn = [aview(8 * KB + i * 2 * KB, 512, F32) for i in range(2)]
        vln = aview(12 * KB, 2048, BF).rearrange("p (b c) -> p b c", c=512)
        pooled = aview(16 * KB, 2048, BF).rearrange("p (c t) -> p c t", t=ST)
        sqrot_c = [aview(20 * KB + i * KB, 512, BF) for i in range(2)]
        smallc = aview(28 * KB, 256, F32)
        pstg = [aview(i * 16 * KB, SLAB, F32) for i in range(2)]
        pbf = [aview(32 * KB + i * 8 * KB, SLAB, BF) for i in range(2)]
        cstg = aview(8 * KB, 1536, F32)

        def mm(o, lhsT, rhs, start, stop):
            op = pg.add("pe", lambda e: e.matmul(o, lhsT=lhsT, rhs=rhs, start=start, stop=stop),
                        reads=[lhsT, rhs], writes=[o])
            op.label = (op.label, "MATMUL %d*%d*%d" % (lhsT.shape[0], lhsT.shape[1], rhs.shape[-1]))

        def tr(o, in_, idn):
            op = pg.add("pe", lambda e: e.transpose(o, in_, idn), reads=[in_, idn], writes=[o])
            op.label = (op.label, "TR")

        def act(o, in_, func, bias=None, scale=None, accum=None):
            kw = {}
            rd = [in_]
            if bias is not None:
                kw["bias"] = bias
                if not isinstance(bias, float):
                    rd.append(bias)
            if scale is not None:
                kw["scale"] = scale
                if not isinstance(scale, float):
                    rd.append(scale)
            wr = [o]
            if accum is not None:
                kw["accum_out"] = accum
                wr.append(accum)
            pg.add("act", lambda e: e.activation(out=o, in_=in_, func=func, **kw), reads=rd, writes=wr)

        def tt(o, a, b, op, eng="dve"):
            pg.add(eng, lambda e: e.tensor_tensor(out=o, in0=a, in1=b, op=op), reads=[a, b], writes=[o])

        def ts(o, a, s1, op0, s2=None, op1=None, eng="dve"):
            rd = [a]
            if not isinstance(s1, float):
                rd.append(s1)
            if s2 is not None and not isinstance(s2, float):
                rd.append(s2)
            if op1 is None:
                pg.add(eng, lambda e: e.tensor_scalar(out=o, in0=a, scalar1=s1, scalar2=None, op0=op0),
                       reads=rd, writes=[o])
            else:
                pg.add(eng, lambda e: e.tensor_scalar(out=o, in0=a, scalar1=s1, scalar2=s2, op0=op0, op1=op1),
                       reads=rd, writes=[o])

        def stt(o, a, s, b, op0, op1):
            rd = [a, b]
            if not isinstance(s, float):
                rd.append(s)
            pg.add("dve", lambda e: e.scalar_tensor_tensor(out=o, in0=a, scalar=s, in1=b, op0=op0, op1=op1),
                   reads=rd, writes=[o])

        def cp(o, a, eng="dve"):
            if eng == "act":
                pg.add("act", lambda e: e.activation(out=o, in_=a, func=AF.Copy), reads=[a], writes=[o])
            else:
                pg.add(eng, lambda e: e.tensor_copy(out=o, in_=a), reads=[a], writes=[o])

        def memset(o, val, eng="dve"):
            pg.add(eng, lambda e: e.memset(o, val), reads=[], writes=[o])

        def recip(o, a):
            pg.add("dve", lambda e: e.reciprocal(out=o, in_=a), reads=[a], writes=[o])

        def reduce_max(o, a):
            pg.add("dve", lambda e: e.tensor_reduce(out=o, in_=a, axis=AX.X, op=ALU.max), reads=[a], writes=[o])

        def dma(o, in_, q="sp", rd_track=True, is_out=False):
            reads = [in_] if rd_track else []
            if q == "sp":
                pg.add("sp", lambda e: e.dma_start(out=o, in_=in_), reads=reads, writes=[o], dma=True, is_out=is_out)
            elif q == "pool":
                pg.add("pool", lambda e: e.dma_start(out=o, in_=in_), reads=reads, writes=[o], dma=True, is_out=is_out)
            else:
                pg.add("act", lambda e: e.dma_start(out=o, in_=in_), reads=reads, writes=[o], dma=True, is_out=is_out)

        def col(c):
            return cols[:, c:c + 1]

        bank = lambda b: psum[:, b, :]
        _rr = {"i": 0}

        def nbank(pool):
            _rr["i"] += 1
            return pool[_rr["i"] % len(pool)]

        POOL_A = [0, 1, 2]
        POOL_F = [0, 1, 2, 3, 4, 5, 6]
        B_S0, B_PT, B_O, B_ST = 3, 5, 6, 7

        dma(cols[:], cols_d, rd_track=False)
        dma(bcs[:], bc_d, rd_track=False)
        dma(cstg[:, 0:640], cst_d[:, 0:640], rd_track=False)
        cp(ident[:], cstg[:, K_IDENT:K_IDENT + 128])
        cp(identb[:], cstg[:, K_IDENT:K_IDENT + 128])
        cp(mask2[:], cstg[:, K_MASK:K_MASK + 512].rearrange("p (a s) -> p a s", s=256))
        memset(ones_b[:], 1.0)
        memset(onesm_b[:], 1.0 / 512.0)
        ts(nsink[:], cols[:, C_SINK:C_SINK + 8], -1.0, ALU.mult)
        trilf = aview(14 * KB, 128, F32)
        wsf = aview(0, 512, F32).rearrange("p (g t) -> p g t", t=128)
        dma(trilf, cst_d[:, K_TRIL:K_TRIL + 128], rd_track=False)
        dma(wsf, wst_d, rd_track=False)
        for g in range(4):
            tt(wst_b[:, g, :], wsf[:, g, :], trilf, ALU.mult)
        wpf = aview(2 * KB, 512, F32).rearrange("p (g t) -> p g t", t=128)
        dma(wpf, wpool_d, rd_track=False)
        cp(wpool_b[:], wpf)
        rowf = aview(4 * KB, NROW, F32)[0:1, :]
        dma(rowf, rows_d, rd_track=False)
        cp(rows_b[:], rowf)
        dma(cstg[:], cst_d[:, K_POOL:K_POOL + 1536], rd_track=False)
        cp(poolm[:], cstg[:].rearrange("p (m t) -> p m t", t=128))
        memset(col_eps[:], EPS)
        memset(kT[:, 0:128], 0.0)
        memset(vtok[:, 0, :], 0.0)
        memset(zptok[:, 0, :], 0.0)
        dstate = {"i": 0}

        slab_no = {"i": 0}

        slab_src = {}

        def conv_slab(srcs, width):
            j = slab_no["i"]
            slab_no["i"] += 1
            slab_src[j] = (srcs, width)
            return j

        slabs = {}
        w_in0_v = w_in0.rearrange("(k p) f -> p k f", p=128)
        w_out0_v = w_out0.rearrange("(k p) f -> p k f", p=128)
        w_in1_v = w_in1.rearrange("(k p) f -> p k f", p=128)
        w_out1_v = w_out1.rearrange("(k p) f -> p k f", p=128)
        for j in range(4):
            nco = 512 if j < 3 else 256
            slabs[("in0", j)] = conv_slab([(8, 0, nco, w_in0_v[:, :, j * 512:j * 512 + nco])], nco)
        for j in range(2):
            slabs[("out0", j)] = conv_slab([(8, 0, 512, w_out0_v[:, :, j * 512:(j + 1) * 512])], 512)

        def ffn_slabs(l):
            wg_v = wg[l].rearrange("(k p) f -> p k f", p=128)
            wu_v = wu[l].rearrange("(k p) f -> p k f", p=128)
            wd_v = wd[l].rearrange("(k p) f -> p k f", p=128)
            for j in range(11):
                srcs = [(8, 0, 256, wg_v[:, :, j * 256:(j + 1) * 256]),
                        (8, 256, 256, wu_v[:, :, j * 256:(j + 1) * 256])]
                slabs[("gu%d" % l, j)] = conv_slab(srcs, 512)
            for j in range(8):
                slabs[("dn%d" % l, j)] = conv_slab([(NF, 0, 128, wd_v[:, :, j * 128:(j + 1) * 128])], 128)


        ffn_slabs(0)
        for j, c0 in enumerate((0, 1024, 512)):
            slabs[("in1", j)] = conv_slab([(8, 0, 512, w_in1_v[:, :, c0:c0 + 512])], 512)
        for j in range(2):
            slabs[("out1", j)] = conv_slab([(8, 0, 512, w_out1_v[:, :, j * 512:(j + 1) * 512])], 512)
        ffn_slabs(1)
        assert slab_no["i"] == NSLAB
        slab_elems = {}
        for (nm, j), idx in slabs.items():
            if nm.startswith("in0"):
                slab_elems[idx] = 8 * (512 if j < 3 else 256)
            elif nm.startswith("dn"):
                slab_elems[idx] = NF * 128
            else:
                slab_elems[idx] = 8 * 512
        order = [("in0", j) for j in range(4)] + [("out0", j) for j in range(2)] + \
                [("gu0", j) for j in range(11)] + [("dn0", j) for j in range(8)] + \
                [("in1", j) for j in range(3)] + [("out1", j) for j in range(2)] + \
                [("gu1", j) for j in range(11)] + [("dn1", j) for j in range(8)]
        seq_slabs = [slabs[k] for k in order]
        total_uses = ntiles * NSLAB
        wstate = {"next_load": NSLAB, "next_use": 0, "next_conv": 0}

        def issue_load():
            q = wstate["next_load"]
            if q >= total_uses:
                return
            wstate["next_load"] = q + 1
            idx = seq_slabs[q % NSLAB]
            n = slab_elems[idx]
            dma(ring[:, q % RING, 0:n], wscr[idx, :, 0:n], q="sp")

        def issue_conv():
            q = wstate["next_conv"]
            wstate["next_conv"] = q + 1
            idx = seq_slabs[q]
            srcs, width = slab_src[idx]
            kk = srcs[0][0]
            dst = ring[:, q % 2, :]
            k0 = 0
            for hf in range(2):
                nk = (kk + 1) // 2 if hf == 0 else kk - (kk + 1) // 2
                stg = ring[:, 2 + hf, :].bitcast(F32)
                for (kk_, c0, ncol, src) in srcs:
                    d_ = stg[:, 0:nk * width].rearrange("p (k c) -> p k c", c=width)[:, :, c0:c0 + ncol]
                    dma(d_, src[:, k0:k0 + nk, :], q="sp", rd_track=False)
                if hf == 0:
                    cp(dst[:, k0 * width:(k0 + nk) * width], stg[:, 0:nk * width], eng="pool")
                else:
                    wstate["pend"] = (q, dst[:, k0 * width:(k0 + nk) * width], stg[:, 0:nk * width], idx, dst)
                k0 += nk

        def finish_conv(q):
            pq, d_, s_, idx, dst = wstate.pop("pend")
            assert pq == q
            cp(d_, s_, eng="dve")
            n = slab_elems[idx]
            dma(wscr[idx, :, 0:n], dst[:, 0:n], q="pool")

        def use_slab(key):
            q = wstate["next_use"]
            wstate["next_use"] = q + 1
            assert seq_slabs[q % NSLAB] == slabs[key], (key, q)
            if q < NSLAB:
                if q == 0:
                    issue_conv()
                finish_conv(q)
                if q + 1 < NSLAB:
                    issue_conv()
                if q >= NSLAB - 1:
                    while wstate["next_load"] < min(q + RING, total_uses):
                        issue_load()
                return ring[:, q % 2, :]
            while wstate["next_load"] < min(q + RING, total_uses):
                issue_load()
            return ring[:, q % RING, :]

        def load_x(ti):
            s_, st_ = divmod(ti, nst_seq)
            for b in range(NB):
                t0 = st_ * ST + b * 128
                dma(stg_in[b][:], x[s_, t0:t0 + 128, :], q="pool", rd_track=False)

        class StatAcc:
            def __init__(self, bufs):
                self.bufs = bufs
                self.n = 0
                self.pend = None

            def _flush(self, last):
                if self.pend is not None:
                    i, b = self.pend
                    mm(bank(B_ST), ones_b[:], b, i == 0, last)
                    self.pend = None

            def add(self, src):
                b = self.bufs[self.n % 2]
                act(b, src, AF.Square)
                self._flush(False)
                self.pend = (self.n, b)
                self.n += 1

            def finish(self):
                assert self.n == KD
                self._flush(True)

        def norm_finish(hh, gcol0, hno, bst=None):
            bst = B_ST if bst is None else bst
            act(rstd_s, bank(bst), AF.Sqrt, bias=col_eps[:], scale=1.0 / D)
            recip(bank(bst), rstd_s)
            for k in range(KD):
                stt(hno[:, k, :], hh[:, k, :], col(gcol0 + k), bank(bst), ALU.mult, ALU.mult)

        def proj_resid(hh, src, slab_keys, sqbufs):
            sa = StatAcc(sqbufs)
            for j, key in enumerate(slab_keys):
                w = use_slab(key).rearrange("p (k c) -> p k c", c=512)
                for cc in range(4):
                    c = j * 4 + cc
                    bk = nbank(POOL_A)
                    for k in range(KD):
                        mm(bank(bk), w[:, k, cc * 128:(cc + 1) * 128], src[:, k, :], k == 0, k == KD - 1)
                    tt(hh[:, c, :], bank(bk), hh[:, c, :], ALU.add)
                    sa.add(hh[:, c, :])
            sa.finish()

        def ffn_gu(hh, l, hnb, hook=None, pre_hook=None):
            pg.label = "F%d_norm" % l
            norm_finish(hh, C_GFFN0 if l == 0 else C_GFFN1, hnb)
            if pre_hook is not None:
                pre_hook()
            pg.label = "F%d_gu" % l
            for j in range(11):
                w = use_slab(("gu%d" % l, j)).rearrange("p (k c) -> p k c", c=512)
                for i in range(2):
                    f_ = 2 * j + i
                    bg = nbank(POOL_F)
                    for k in range(KD):
                        mm(bank(bg), w[:, k, i * 128:i * 128 + 128], hnb[:, k, :], k == 0, k == KD - 1)
                    bu = nbank(POOL_F)
                    for k in range(KD):
                        mm(bank(bu), w[:, k, 256 + i * 128:256 + i * 128 + 128], hnb[:, k, :], k == 0, k == KD - 1)
                    sg_ = sgt[f_ % 4]
                    act(sg_, bank(bg), AF.Silu)
                    tt(actb[:, f_, :], bank(bu), sg_, ALU.mult)
                if hook is not None and j == 7:
                    hook()
                    pg.label = "F%d_gu" % l

        def ffn_dn(hh, l):
            pg.label = "F%d_dn" % l
            sa = StatAcc(sqrot_f)
            for j in range(8):
                w = use_slab(("dn%d" % l, j)).rearrange("p (f c) -> p f c", c=128)
                bk = nbank(POOL_F)
                for f_ in range(NF):
                    mm(bank(bk), w[:, f_, :], actb[:, f_, :], f_ == 0, f_ == NF - 1)
                tt(hh[:, j, :], bank(bk), hh[:, j, :], ALU.add)
                sa.add(hh[:, j, :])
            sa.finish()

        def gen_diag(cc):
            slots = []
            for j in range(31):
                dslot = dstate["i"] % NDIAG
                dstate["i"] += 1
                ts(diagbuf[:, dslot, :], identb[:], col(C_CONVW + cc * 31 + j), ALU.mult, 0.0, ALU.add,
                   eng="pool")
                slots.append(dslot)
            return slots

        def A_pre(ti, hh, hnb, bpool):
            pg.label = "A1_xT"
            pend = None

            def blk_stats(b):
                for k in range(KD):
                    mm(psum[:, B_O, b * 128:(b + 1) * 128], ones_b[:], sqblk[b % 2][:, k, :], k == 0, k == KD - 1)

            for b in range(NB):
                if b >= 2:
                    blk_stats(b - 2)
                for half in range(2):
                    bk = nbank(bpool)
                    for kk in range(4):
                        k = half * 4 + kk
                        tr(psum[:, bk, kk * 128:(kk + 1) * 128], stg_in[b][:, k * 128:(k + 1) * 128], ident[:])
                    o = hh[:, half * 4:half * 4 + 4, b * 128:(b + 1) * 128]
                    i_ = bank(bk).rearrange("p (k t) -> p k t", t=128)
                    if b >= 2:
                        cp(o, i_, eng="dve")
                    else:
                        cp(o, i_, eng="act")
                    act(sqblk[b % 2][:, half * 4:half * 4 + 4, :], o, AF.Square)
            blk_stats(2)
            blk_stats(3)
            if dbg and "h_in" in dbg_out and ti == 0:
                dma(dbg_out["h_in"], hh[:], q="sp")
            pg.label = "A2_norm"
            norm_finish(hh, C_GMIX0, hnb, bst=B_O)

        def A_main(ti, hh, hnb):
            s_, st_ = divmod(ti, nst_seq)
            first = st_ == 0
            dslots = {0: gen_diag(0)}
            pg.label = "A4_inproj"
            if first:
                memset(glu[:, :, 0:30], 0.0)
            else:
                cp(kT[:, 0:128], kT[:, ST:ST + 128], eng="pool")
                cp(vtok[:, 0, :], vtok[:, NB, :], eng="pool")
                cp(glu[:, :, 0:30], glu[:, :, ST:ST + 30], eng="pool")
            w = use_slab(("in0", 0)).rearrange("p (k c) -> p k c", c=512)
            for c in range(4):
                bk = nbank(POOL_A)
                for k in range(KD):
                    mm(bank(bk), w[:, k, c * 128:(c + 1) * 128], hnb[:, k, :], k == 0, k == KD - 1)
                act(qT[:, c, :], bank(bk), AF.Identity, bias=col(C_BIN + c))
            w1 = use_slab(("in0", 1)).rearrange("p (k c) -> p k c", c=512)
            bk = nbank(POOL_A)
            for k in range(KD):
                mm(bank(bk), w1[:, k, 0:128], hnb[:, k, :], k == 0, k == KD - 1)
            act(kT[:, 128:128 + ST], bank(bk), AF.Identity, bias=col(C_BIN + 4))
            bk = nbank(POOL_A)
            for b in range(NB):
                o = psum[:, bk, b * 128:(b + 1) * 128]
                for k in range(KD):
                    mm(o, hnb[:, k, b * 128:(b + 1) * 128], w1[:, k, 128:256], k == 0, False)
                mm(o, ones_b[0:1, :], rows_b[0:1, R_BV:R_BV + 128], False, True)
            cp(vtok[:, 1:1 + NB, :], bank(bk).rearrange("p (b c) -> p b c", c=128), eng="act")

            def glu_pair(cc, wa, wg_):
                ba = nbank(POOL_A)
                for k in range(KD):
                    mm(bank(ba), wa[k], hnb[:, k, :], k == 0, k == KD - 1)
                bg = nbank(POOL_A)
                for k in range(KD):
                    mm(bank(bg), wg_[k], hnb[:, k, :], k == 0, k == KD - 1)
                sg_ = sig[cc % 2]
                act(sg_, bank(bg), AF.Sigmoid, bias=col(C_BIN + 7 + 2 * cc))
                stt(glu[:, cc, 30:30 + ST], bank(ba), col(C_BIN + 6 + 2 * cc), sg_, ALU.add, ALU.mult)

            glu_pair(0, [w1[:, k, 256:384] for k in range(KD)], [w1[:, k, 384:512] for k in range(KD)])

            wst_ = {}

            def f_glu1():
                w2 = use_slab(("in0", 2)).rearrange("p (k c) -> p k c", c=512)
                wst_["w2"] = w2
                glu_pair(1, [w2[:, k, 0:128] for k in range(KD)], [w2[:, k, 128:256] for k in range(KD)])

            def f_glu2():
                w2 = wst_["w2"]
                glu_pair(2, [w2[:, k, 256:384] for k in range(KD)], [w2[:, k, 384:512] for k in range(KD)])

            def f_glu3():
                w3 = use_slab(("in0", 3)).rearrange("p (k c) -> p k c", c=256)
                glu_pair(3, [w3[:, k, 0:128] for k in range(KD)], [w3[:, k, 128:256] for k in range(KD)])

            cbank = {}

            def f_conv(cc, half):
                def f():
                    lab = pg.label
                    pg.label = "A6_conv"
                    if half == 0:
                        cbank[cc] = nbank(POOL_A)
                        if cc + 1 < 4:
                            dslots[cc + 1] = gen_diag(cc + 1)
                    bk_ = cbank[cc]
                    for j in (range(0, 16) if half == 0 else range(16, 31)):
                        mm(bank(bk_), diagbuf[:, dslots[cc][j], :], glu[:, cc, j:j + ST], j == 0, j == 30)
                    if half == 1:
                        act(cv[:, cc, :], bank(bk_), AF.Identity, bias=col(C_CONVB + cc))
                        act(cvb[:, cc, :], bank(bk_), AF.Identity, bias=col(C_CONVB + cc))
                        act(cvq[:, cc, :], cv[:, cc, :], AF.Square)
                    pg.label = lab
                return f

            def f_ln():
                lab = pg.label
                pg.label = "A6_conv"
                bm = nbank(POOL_A)
                for cc in range(4):
                    mm(bank(bm), onesm_b[:], cvb[:, cc, :], cc == 0, cc == 3)
                for cc in range(4):
                    mm(bank(B_ST), onesm_b[:], cvq[:, cc, :], cc == 0, cc == 3)
                m2 = lnt[0]
                act(m2, bank(bm), AF.Square)
                tt(m2, bank(B_ST), m2, ALU.subtract)
                act(m2, m2, AF.Sqrt, bias=col_eps[:], scale=1.0)
                recip(bank(B_ST), m2)
                for cc in range(4):
                    d_ = lnt[1 + cc % 2]
                    tt(d_, cv[:, cc, :], bank(bm), ALU.subtract)
                    stt(d_, d_, col(C_CLNG + cc), bank(B_ST), ALU.mult, ALU.mult)
                    act(cat[:, 4 + cc, :], d_, AF.Silu, bias=col(C_CLNB + cc))
                pg.label = lab

            fillers = [f_glu1, f_glu2, f_glu3]
            for cc in range(4):
                fillers += [f_conv(cc, 0), f_conv(cc, 1)]
            fillers.append(f_ln)

            def filler():
                if fillers:
                    fillers.pop(0)()

            pg.label = "A5_attn"
            S4 = psum[:, B_S0:B_S0 + 2, :].rearrange("p a (g s) -> p (a g) s", s=256)

            def att_S(p):
                b, kv = divmod(p, 2)
                ph = kv * 64
                mi = 1 if (first and b == 0) else 0
                mk = mask2[:, mi:mi + 1, :].broadcast_to([128, 4, 256])
                for g in range(4):
                    mm(S4[:, g, :], qT[ph:ph + 64, g, b * 128:(b + 1) * 128],
                       kT[ph:ph + 64, b * 128:b * 128 + 256], True, True)
                sm_ = Sm[p % 2]
                tt(sm_[:, :, :], S4[:, :, :], mk, ALU.add)
                so = (p % 2) * 32
                mx = small[:, so + 0:so + 4]
                negm = small[:, so + 4:so + 8]
                dd = small[:, so + 8:so + 12]
                es_ = small[:, so + 12:so + 16]
                bo = (b % 2) * 16 + kv * 4
                ll = st8[:, bo:bo + 4]
                rinv = st8[:, 32 + bo:32 + bo + 4]
                reduce_max(mx, sm_[:, :, :])
                stt(negm, mx, -0.125, nsink[:, kv * 4:kv * 4 + 4], ALU.mult, ALU.min)
                tt(dd, cols[:, C_SINK + kv * 4:C_SINK + kv * 4 + 4], negm, ALU.add)
                for g in range(4):
                    act(Pb[p % 2][:, g, :], sm_[:, g, :], AF.Exp, bias=negm[:, g:g + 1], scale=0.125,
                        accum=ll[:, g:g + 1])
                act(es_, dd, AF.Exp)
                tt(ll, ll, es_, ALU.add)
                recip(rinv, ll)

            def att_T(p):
                PTp = bank(B_PT).bitcast(BF).rearrange("p (a g q) -> p a g q", g=4, q=128)
                for a in range(2):
                    for g in range(4):
                        tr(PTp[:, a, g, :], Pb[p % 2][:, g, a * 128:(a + 1) * 128], identb[:])
                cp(PT[p % 2][:], bank(B_PT).bitcast(BF).rearrange("p (a c) -> p a c", c=512), eng="act")

            def att_PV(p):
                b, kv = divmod(p, 2)
                ph = kv * 64
                for g in range(4):
                    o = psum[:, B_O, g * 128 + ph:g * 128 + ph + 64]
                    for a in range(2):
                        mm(o, PT[p % 2][:, a, g * 128:(g + 1) * 128], vtok[:, b + a, ph:ph + 64], a == 0, a == 1)
                if kv == 1:
                    bo = (b % 2) * 16
                    rbc = st8[:, 32 + bo:32 + bo + 8].rearrange("p (kv g) -> p g kv", g=4) \
                        .unsqueeze(3).broadcast_to([128, 4, 2, 64])
                    tt(Osb.rearrange("p (g kv d) -> p g kv d", kv=2, d=64),
                       bank(B_O).rearrange("p (g kv d) -> p g kv d", kv=2, d=64), rbc, ALU.mult)
                    bk_ = nbank(POOL_A)
                    ov = bank(bk_).bitcast(BF)[:, 0:512].rearrange("p (g q) -> p g q", q=128)
                    for g in range(4):
                        tr(ov[:, g, :], Osb[:, g * 128:(g + 1) * 128], identb[:])
                    cp(cat[:, 0:4, b * 128:(b + 1) * 128], ov, eng="act")

            NP = 2 * NB
            att_S(0)
            filler()
            filler()
            for p in range(NP):
                if p + 1 < NP:
                    att_S(p + 1)
                filler()
                att_T(p)
                if p % 2 == 1:
                    filler()
                if p > 0:
                    att_PV(p - 1)
            while fillers:
                filler()
            att_PV(NP - 1)
            pg.label = "A7_out"
            proj_resid(hh, cat, [("out0", 0), ("out0", 1)], sqrot_a)

        def phase_C(ti, hh, hnb):
            s_, st_ = divmod(ti, nst_seq)
            first = st_ == 0
            pg.label = "C1_norm"
            norm_finish(hh, C_GMIX1, hnb)
            pg.label = "C2_in"
            if not first:
                cp(zptok[:, 0, :], zptok[:, NB, :], eng="pool")
            w = use_slab(("in1", 0)).rearrange("p (k c) -> p k c", c=512)
            for b in range(NB):
                bk = nbank(POOL_A)
                for k in range(KD):
                    mm(bank(bk), hnb[:, k, b * 128:(b + 1) * 128], w[:, k, :], k == 0, k == KD - 1)
                cp(zptok[:, 1 + b, :], bank(bk), eng=("act" if b % 2 else "dve"))
            wv = use_slab(("in1", 1)).rearrange("p (k c) -> p k c", c=512)
            for b in range(NB):
                bk = nbank(POOL_A)
                for k in range(KD):
                    mm(bank(bk), hnb[:, k, b * 128:(b + 1) * 128], wv[:, k, :], k == 0, k == KD - 1)
                vg_ = vg[b % 2]
                vn_ = vn[b % 2]
                act(vg_, bank(bk), AF.Gelu_apprx_tanh)
                st6 = smallc[:, b * 16:b * 16 + 6]
                mv = smallc[:, b * 16 + 8:b * 16 + 10]
                sd = smallc[:, b * 16 + 10:b * 16 + 11]
                pg.add("dve", lambda e, o=st6, i_=vg_: e.bn_stats(out=o, in_=i_), reads=[vg_], writes=[st6])
                pg.add("dve", lambda e, o=mv, i_=st6: e.bn_aggr(out=o, in_=i_), reads=[st6], writes=[mv])
                act(sd, mv[:, 1:2], AF.Sqrt, bias=col_eps[:], scale=1.0)
                recip(sd, sd)
                ts(vn_, vg_, mv[:, 0:1], ALU.subtract, sd, ALU.mult)
                tt(vn_, vn_, bcs[:, 0, :], ALU.mult, eng="pool")
                tt(vln[:, b, :], vn_, bcs[:, 1, :], ALU.add, eng="pool")
            w = use_slab(("in1", 2)).rearrange("p (k c) -> p k c", c=512)
            for c in range(4):
                bk = nbank(POOL_A)
                for k in range(KD):
                    mm(bank(bk), w[:, k, c * 128:(c + 1) * 128], hnb[:, k, :], k == 0, k == KD - 1)
                act(uT[:, c, :], bank(bk), AF.Gelu_apprx_tanh)
            pg.label = "C3_pool"
            for g in range(4):
                bk = nbank(POOL_A)
                for b in range(NB):
                    o = psum[:, bk, b * 128:(b + 1) * 128]
                    if first and b == 0:
                        mm(o, zptok[:, 1 + b, g * 128:(g + 1) * 128], poolm[:, 8 + g, :], True, True)
                    else:
                        mm(o, zptok[:, 1 + b, g * 128:(g + 1) * 128], poolm[:, g, :], True, False)
                        mm(o, zptok[:, b, g * 128:(g + 1) * 128], poolm[:, 4 + g, :], False, True)
                cp(pooled[:, g, :], bank(bk), eng=("act" if g % 2 else "dve"))
            for g in range(4):
                bk2 = nbank(POOL_A)
                mm(bank(bk2), wpool_b[:, g, :], pooled[:, g, :], True, True)
                act(cat[:, g, :], bank(bk2), AF.Identity, scale=col(C_PSCALE + g))
            pg.label = "C4_sgu"
            for g in range(4):
                bk = nbank(POOL_A)
                for b in range(NB):
                    o = psum[:, bk, b * 128:(b + 1) * 128]
                    mm(o, vln[:, b, g * 128:(g + 1) * 128], wst_b[:, g, :], True, False)
                    mm(o, ones_b[0:1, :], rows_b[0:1, R_BS + g * 128:R_BS + (g + 1) * 128], False, True)
                tt(cat[:, 4 + g, :], bank(bk), uT[:, g, :], ALU.mult)
            pg.label = "C5_out"
            proj_resid(hh, cat, [("out1", 0), ("out1", 1)], sqrot_c)

        def E_pre(ti, hh):
            pg.label = "E_final"
            act(rstd_s, bank(B_ST), AF.Sqrt, bias=col_eps[:], scale=1.0 / D)
            recip(bank(B_ST), rstd_s)
            for k in range(KD):
                stt(hh[:, k, :], hh[:, k, :], col(C_GFIN + k), bank(B_ST), ALU.mult, ALU.mult)

        def E_post(ti, hh):
            s_, st_ = divmod(ti, nst_seq)
            pg.label = "E_final"
            for b in range(NB):
                so = stg_out[b % 2]
                for half in range(2):
                    bk = nbank(POOL_F)
                    for kk in range(4):
                        k = half * 4 + kk
                        tr(psum[:, bk, kk * 128:(kk + 1) * 128], hh[:, k, b * 128:(b + 1) * 128], ident[:])
                    if half == 0:
                        cp(so[:, 0:512], bank(bk), eng="act")
                    else:
                        cp(so[:, 512:1024], bank(bk), eng="dve")
                t0 = st_ * ST + b * 128
                dma(out[s_, t0:t0 + 128, :], so[:], q="pool", is_out=True)

        load_x(0)
        A_pre(0, h[0], hn[0], POOL_A)
        for ti in range(ntiles):
            hh = h[ti % 2]
            if ti + 1 < ntiles:
                load_x(ti + 1)
            A_main(ti, hh, hn[0])
            if dbg and "h_a" in dbg_out and ti == 0:
                dma(dbg_out["h_a"], hh[:], q="sp")
            ffn_gu(hh, 0, hn[1], pre_hook=((lambda t=ti: E_post(t - 1, h[(t - 1) % 2])) if ti > 0 else None))
            ffn_dn(hh, 0)
            if dbg and "h_b" in dbg_out and ti == 0:
                dma(dbg_out["h_b"], hh[:], q="sp")
            phase_C(ti, hh, hn[0])
            if dbg and "h_c" in dbg_out and ti == 0:
                dma(dbg_out["h_c"], hh[:], q="sp")
            ffn_gu(hh, 1, hn[1],
                   pre_hook=((lambda t=ti: A_pre(t + 1, h[(t + 1) % 2], hn[0], [0, 1, 2, 3, 4, 5])) if ti + 1 < ntiles else None))
            ffn_dn(hh, 1)
            E_pre(ti, hh)
        E_post(ntiles - 1, h[(ntiles - 1) % 2])
        pg.emit_all(block, sems, dsems)
    return nc, pg


_CACHE = {}


def kernel(**inputs):
    xs = np.asarray(inputs["x"])
    B, S, _ = xs.shape
    nseq = B // NCORES
    maps = prep_inputs(inputs, NCORES, nseq)
    key = (nseq, S)
    if key not in _CACHE:
        _CACHE[key] = build(nseq, S)[0]
    nc = _CACHE[key]
    res = run_bass_kernel_spmd(nc, maps, core_ids=list(range(NCORES)))
    outs = [np.asarray(r["out"]) for r in res.results]
    return np.concatenate(outs, axis=0).astype(np.float32)
```
